# Optimizing a Trainium2 kernel written in Bass

```python
import math
import jax, jax.numpy as jnp
from jax import lax
import numpy as np

D_MODEL = 2048
BATCH = 4
SEQ = 8192
DEPTH = 1
DEC_BATCH = 32
DEC_SEQ = 64
PAST_LEN = 1024

CHUNK = 64
Q_BLOCK = 128
N_A = 8
DA_HEAD = 64
DA_V = 2 * DA_HEAD
N_B = 8
DB_HEAD = 128
ROT_DIM = DA_HEAD // 4
ROPE_THETA = 500000.0
D_FF = 5504
EPS = 1e-6
W_QA = N_A * 2 * DA_HEAD
W_VA = N_A * DA_V
W_B = N_B * DB_HEAD
IN_COLS = 2 * W_QA + W_VA + 3 * W_B + 2 * D_MODEL

kernel_name = "hybrid_diffattn_stickbreaking_streaming_step"


def rmsnorm(x, g):
    xf = x.astype(jnp.float32)
    y = xf * lax.rsqrt(jnp.mean(xf * xf, axis=-1, keepdims=True) + EPS)
    return (y * g.astype(jnp.float32)).astype(x.dtype)


def swiglu(x, wg, wu, wd):
    return (jax.nn.silu(x @ wg) * (x @ wu)) @ wd


def rope_partial(x, pos):
    half = ROT_DIM // 2
    inv_freq = jnp.power(ROPE_THETA, -jnp.arange(0, ROT_DIM, 2, dtype=jnp.float32) / ROT_DIM)
    ang = pos.astype(jnp.float32)[:, None] * inv_freq[None, :]
    cos = jnp.cos(ang)[None, :, None, None, :]
    sin = jnp.sin(ang)[None, :, None, None, :]
    xr = x[..., :ROT_DIM].astype(jnp.float32)
    x1, x2 = xr[..., :half], xr[..., half:]
    rot = jnp.concatenate([x1 * cos - x2 * sin, x2 * cos + x1 * sin], axis=-1)
    return jnp.concatenate([rot.astype(x.dtype), x[..., ROT_DIM:]], axis=-1)


def diff_attn(q, k, v, q_pos, k_pos, lam, subln_g, lam_init):
    b, tq = q.shape[0], q.shape[1]
    s = jnp.einsum('bqhmd,bkhmd->bhmqk', q, k).astype(jnp.float32) * (DA_HEAD ** -0.5)
    mask = (k_pos[None, :] // CHUNK) <= (q_pos[:, None] // CHUNK)
    s = jnp.where(mask[None, None, None], s, -1e30)
    p = jax.nn.softmax(s, axis=-1)
    w = p[:, :, 0] - lam * p[:, :, 1]
    o = jnp.einsum('bhqk,bkhe->bqhe', w.astype(v.dtype), v)
    o = rmsnorm(o, subln_g) * (1.0 - lam_init)
    return o.reshape(b, tq, W_VA)


def stick_breaking_attn(q, k, v, q_pos, k_pos):
    b, tq = q.shape[0], q.shape[1]
    z = jnp.einsum('bqhd,bkhd->bhqk', q, k).astype(jnp.float32) * (DB_HEAD ** -0.5)
    mask = (k_pos[None, :] < q_pos[:, None])[None, None]
    u = jnp.where(mask, jax.nn.log_sigmoid(-z), 0.0)
    rest = lax.cumsum(u, axis=3, reverse=True) - u
    a = jnp.where(mask, jnp.exp(jax.nn.log_sigmoid(z) + rest), 0.0)
    o = jnp.einsum('bhqk,bkhd->bqhd', a.astype(v.dtype), v)
    return o.reshape(b, tq, W_B)


def sweep_queries(fn, q, q_pos):
    b, t = q.shape[0], q.shape[1]
    nb = t // Q_BLOCK
    qb = q.reshape((b, nb, Q_BLOCK) + q.shape[2:]).swapaxes(0, 1)
    pb = q_pos.reshape(nb, Q_BLOCK)
    out = lax.map(lambda a: fn(a[0], a[1]), (qb, pb))
    return out.swapaxes(0, 1).reshape(b, t, out.shape[-1])


def layer(x, pos, past, blocked, lam_init, w_in, w_up_a, w_up_b, w_o, lq1, lk1, lq2, lk2, subln_g,
          n1a, n1b, nma, nmb, n2a, n2b, f1g, f1u, f1d, f2g, f2u, f2d):
    b, t, _ = x.shape
    h = x + 0.5 * rmsnorm(swiglu(rmsnorm(x, n1a), f1g, f1u, f1d), n1b)
    n = rmsnorm(h, nma)
    proj = n @ w_in
    cuts = [W_QA, 2 * W_QA, 2 * W_QA + W_VA, 2 * W_QA + W_VA + W_B,
            2 * W_QA + W_VA + 2 * W_B, 2 * W_QA + W_VA + 3 * W_B, 2 * W_QA + W_VA + 3 * W_B + D_MODEL]
    q_a, k_a, v_a, q_b, k_b, v_b, g_a, g_b = jnp.split(proj, cuts, axis=-1)
    q_a = rope_partial(q_a.reshape(b, t, N_A, 2, DA_HEAD), pos)
    k_a = rope_partial(k_a.reshape(b, t, N_A, 2, DA_HEAD), pos)
    v_a = v_a.reshape(b, t, N_A, DA_V)
    q_b = q_b.reshape(b, t, N_B, DB_HEAD)
    k_b = k_b.reshape(b, t, N_B, DB_HEAD)
    v_b = v_b.reshape(b, t, N_B, DB_HEAD)
    new_rows = (k_a, v_a, k_b, v_b)
    if past is None:
        ka_all, va_all, kb_all, vb_all, k_pos = k_a, v_a, k_b, v_b, pos
    else:
        pka, pva, pkb, pvb = past
        ka_all = jnp.concatenate([pka, k_a], axis=1)
        va_all = jnp.concatenate([pva, v_a], axis=1)
        kb_all = jnp.concatenate([pkb, k_b], axis=1)
        vb_all = jnp.concatenate([pvb, v_b], axis=1)
        k_pos = jnp.concatenate([jnp.arange(pka.shape[1], dtype=jnp.int32), pos])
    lq1f, lk1f, lq2f, lk2f = (a.astype(jnp.float32) for a in (lq1, lk1, lq2, lk2))
    lam = jnp.exp(jnp.sum(lq1f * lk1f)) - jnp.exp(jnp.sum(lq2f * lk2f)) + lam_init
    fa = lambda qq, pp: diff_attn(qq, ka_all, va_all, pp, k_pos, lam, subln_g, lam_init)
    fb = lambda qq, pp: stick_breaking_attn(qq, kb_all, vb_all, pp, k_pos)
    if blocked:
        o_a = sweep_queries(fa, q_a, pos)
        o_b = sweep_queries(fb, q_b, pos)
    else:
        o_a = fa(q_a, pos)
        o_b = fb(q_b, pos)
    merged = jax.nn.sigmoid(g_a) * (o_a @ w_up_a) + jax.nn.sigmoid(g_b) * (o_b @ w_up_b)
    h = h + rmsnorm(merged @ w_o, nmb)
    y = h + 0.5 * rmsnorm(swiglu(rmsnorm(h, n2a), f2g, f2u, f2d), n2b)
    return y, new_rows


def setup_inputs(seed: int = 0) -> dict:
    key = jax.random.key(seed)
    ks = jax.random.split(key, 32)
    f32 = jnp.float32
    nrm = lambda k, shape, s: jax.random.normal(k, shape, f32) * s
    gain = lambda k, shape: 1.0 + 0.02 * jax.random.normal(k, shape, f32)
    return {
        "x_prompt": nrm(ks[0], (BATCH, SEQ, D_MODEL), 1.0),
        "x_sample": nrm(ks[1], (DEC_BATCH, DEC_SEQ, D_MODEL), 1.0),
        "cache_diff_k": nrm(ks[2], (DEPTH, DEC_BATCH, PAST_LEN, N_A, 2, DA_HEAD), 1.0),
        "cache_diff_v": nrm(ks[3], (DEPTH, DEC_BATCH, PAST_LEN, N_A, DA_V), 1.0),
        "cache_sb_k": nrm(ks[4], (DEPTH, DEC_BATCH, PAST_LEN, N_B, DB_HEAD), 1.0),
        "cache_sb_v": nrm(ks[5], (DEPTH, DEC_BATCH, PAST_LEN, N_B, DB_HEAD), 1.0),
        "w_in": nrm(ks[6], (DEPTH, D_MODEL, IN_COLS), D_MODEL ** -0.5),
        "w_up_a": nrm(ks[7], (DEPTH, W_VA, D_MODEL), W_VA ** -0.5),
        "w_up_b": nrm(ks[8], (DEPTH, W_B, D_MODEL), W_B ** -0.5),
        "w_o": nrm(ks[9], (DEPTH, D_MODEL, D_MODEL), D_MODEL ** -0.5),
        "lam_q1": nrm(ks[10], (DEPTH, DA_HEAD), 0.1),
        "lam_k1": nrm(ks[11], (DEPTH, DA_HEAD), 0.1),
        "lam_q2": nrm(ks[12], (DEPTH, DA_HEAD), 0.1),
        "lam_k2": nrm(ks[13], (DEPTH, DA_HEAD), 0.1),
        "subln_g": gain(ks[14], (DEPTH, DA_V)),
        "norm_ffn1_pre": gain(ks[15], (DEPTH, D_MODEL)),
        "norm_ffn1_post": gain(ks[16], (DEPTH, D_MODEL)),
        "norm_mix_pre": gain(ks[17], (DEPTH, D_MODEL)),
        "norm_mix_post": gain(ks[18], (DEPTH, D_MODEL)),
        "norm_ffn2_pre": gain(ks[19], (DEPTH, D_MODEL)),
        "norm_ffn2_post": gain(ks[20], (DEPTH, D_MODEL)),
        "ffn1_w_gate": nrm(ks[21], (DEPTH, D_MODEL, D_FF), D_MODEL ** -0.5),
        "ffn1_w_up": nrm(ks[22], (DEPTH, D_MODEL, D_FF), D_MODEL ** -0.5),
        "ffn1_w_down": nrm(ks[23], (DEPTH, D_FF, D_MODEL), D_FF ** -0.5),
        "ffn2_w_gate": nrm(ks[24], (DEPTH, D_MODEL, D_FF), D_MODEL ** -0.5),
        "ffn2_w_up": nrm(ks[25], (DEPTH, D_MODEL, D_FF), D_MODEL ** -0.5),
        "ffn2_w_down": nrm(ks[26], (DEPTH, D_FF, D_MODEL), D_FF ** -0.5),
    }


def reference(x_prompt, x_sample, cache_diff_k, cache_diff_v, cache_sb_k, cache_sb_v,
              w_in, w_up_a, w_up_b, w_o, lam_q1, lam_k1, lam_q2, lam_k2, subln_g,
              norm_ffn1_pre, norm_ffn1_post, norm_mix_pre, norm_mix_post, norm_ffn2_pre, norm_ffn2_post,
              ffn1_w_gate, ffn1_w_up, ffn1_w_down, ffn2_w_gate, ffn2_w_up, ffn2_w_down):
    pos_p = jnp.arange(x_prompt.shape[1], dtype=jnp.int32)
    pos_s = cache_diff_k.shape[2] + jnp.arange(x_sample.shape[1], dtype=jnp.int32)
    hp, hs = x_prompt, x_sample
    rows_p, rows_s = [], []
    for l in range(DEPTH):
        lam_init = 0.8 - 0.6 * math.exp(-0.3 * l)
        params = (w_in[l], w_up_a[l], w_up_b[l], w_o[l], lam_q1[l], lam_k1[l], lam_q2[l], lam_k2[l], subln_g[l],
                  norm_ffn1_pre[l], norm_ffn1_post[l], norm_mix_pre[l], norm_mix_post[l],
                  norm_ffn2_pre[l], norm_ffn2_post[l],
                  ffn1_w_gate[l], ffn1_w_up[l], ffn1_w_down[l], ffn2_w_gate[l], ffn2_w_up[l], ffn2_w_down[l])
        hp, rp = layer(hp, pos_p, None, True, lam_init, *params)
        past = (cache_diff_k[l], cache_diff_v[l], cache_sb_k[l], cache_sb_v[l])
        hs, rs = layer(hs, pos_s, past, False, lam_init, *params)
        rows_p.append(rp)
        rows_s.append(rs)
    pk_a = jnp.stack([r[0] for r in rows_p])
    pv_a = jnp.stack([r[1] for r in rows_p])
    pk_b = jnp.stack([r[2] for r in rows_p])
    pv_b = jnp.stack([r[3] for r in rows_p])
    sk_a = jnp.stack([r[0] for r in rows_s])
    sv_a = jnp.stack([r[1] for r in rows_s])
    sk_b = jnp.stack([r[2] for r in rows_s])
    sv_b = jnp.stack([r[3] for r in rows_s])
    return (hp, hs, pk_a, pv_a, pk_b, pv_b, sk_a, sv_a, sk_b, sv_b)
```

```python
import math
from contextlib import ExitStack

import numpy as np
import concourse.bass as bass
import concourse.mybir as mybir
from concourse.bass_utils import run_bass_kernel_spmd

F32 = mybir.dt.float32
BF16 = mybir.dt.bfloat16
AF = mybir.ActivationFunctionType
ALU = mybir.AluOpType
AX = mybir.AxisListType

D = 2048
KC = 16
DFF = 5504
NCF = 43
HALF0 = 22
NT = 512
EPS = 1e-6
LAM_INIT = 0.8 - 0.6 * math.exp(0.0)
NEG_D = -30000.0
NEG_S = -1024.0

CFG = dict(nb_own=8, nb_aux=8, do_smp=True)


class Buf:
    __slots__ = ("name", "writers", "readers", "overlaps", "sem", "ndma", "queue")

    def __init__(self, name):
        self.name = name
        self.writers = []
        self.readers = []
        self.overlaps = []
        self.sem = None
        self.ndma = 0
        self.queue = None


def overlap(a, b):
    a.overlaps.append(b)
    b.overlaps.append(a)


class Op:
    __slots__ = ("eng", "fn", "deps", "dma", "mark", "val")

    def __init__(self, eng, fn, deps, dma):
        self.eng = eng
        self.fn = fn
        self.deps = deps
        self.dma = dma
        self.mark = False
        self.val = 0


class Prog:
    def __init__(self):
        self.ops = {e: [] for e in ("pe", "act", "dve", "pool", "sp")}
        self.dma_bufs = []

    def op(self, eng, fn, reads=(), writes=(), dma=None):
        deps = set()
        for b in reads:
            deps.update(b.writers)
            for o in b.overlaps:
                deps.update(o.writers)
        for b in writes:
            deps.update(b.writers)
            deps.update(b.readers)
            for o in b.overlaps:
                deps.update(o.writers)
                deps.update(o.readers)
        op = Op(eng, fn, deps, dma)
        for b in reads:
            b.readers.append(op)
        for b in writes:
            b.writers = [op]
            b.readers = []
        if dma is not None:
            if dma.queue is None:
                dma.queue = eng
                self.dma_bufs.append(dma)
            assert dma.queue == eng, (dma.name, dma.queue, eng)
            dma.ndma += 1
            op.val = 16 * dma.ndma
        self.ops[eng].append(op)
        return op

    def finalize(self):
        for e, lst in self.ops.items():
            for op in lst:
                for d in op.deps:
                    d.mark = True
        for e, lst in self.ops.items():
            n = 0
            for op in lst:
                if op.dma is None and op.mark:
                    n += 1
                    op.val = n


def build_program(cfg):
    NB_OWN, NB_AUX, DO_SMP = cfg["nb_own"], cfg["nb_aux"], cfg["do_smp"]
    nc = bass.Bass("TRN2", target_bir_lowering=False)
    P = Prog()
    es = ExitStack()

    def din(name, shape, dt=F32):
        return nc.dram_tensor(name, list(shape), dt, kind="ExternalInput").ap()

    def dout(name, shape, dt=F32):
        return nc.dram_tensor(name, list(shape), dt, kind="ExternalOutput").ap()

    def dscr(name, shape, dt=BF16):
        return nc.dram_tensor(name, list(shape), dt).ap()

    xa = din("xa", [4096, D])
    xo = din("xo", [4096, D])
    xs = din("xs", [256, D])
    cos_t = din("cos_t", [8448, 128])
    sin_t = din("sin_t", [8448, 128])
    ck_a = din("ck_a", [4, 1024, 1024])
    cv_a = din("cv_a", [4, 1024, 1024])
    ck_b = din("ck_b", [4, 1024, 1024])
    cv_b = din("cv_b", [4, 1024, 1024])
    w_in = din("w_in", [D, 10240])
    w_up_a = din("w_up_a", [1024, D])
    w_up_b = din("w_up_b", [1024, D])
    w_o = din("w_o", [D, D])
    fw = {}
    for f in (1, 2):
        fw[f] = (din(f"f{f}g", [D, DFF]), din(f"f{f}u", [D, DFF]), din(f"f{f}d", [DFF, D]))
    gains_d = din("gains", [128, 96])
    subg_d = din("subg", [128, 1])
    lam_d = din("lamv", [128, 256])
    ident_d = din("ident", [128, 128])
    ntri_d = din("ntri", [128, 128])
    maskd_d = din("mask_d", [128, 2048])
    masks_d = din("mask_s", [128, 2048])
    auxb_d = din("auxb", [128, 1])
    y_o = dout("y_o", [4096, D])
    y_s = dout("y_s", [256, D])
    kv_o = [dout(n, [4096, 1024]) for n in ("ka_o", "va_o", "kb_o", "vb_o")]
    kv_s = [dout(n, [256, 1024]) for n in ("ka_s", "va_s", "kb_s", "vb_s")]
    wgu = {f: dscr(f"wgu{f}", [NCF, 128, 2 * KC * 128]) for f in (1, 2)}
    wd = {f: dscr(f"wd{f}", [16, 128, NCF * 128]) for f in (1, 2)}
    wqkv = dscr("wqkv", [12, 2, 128, 8 * 512])
    wmg = dscr("wmg", [16, 128, 48 * 128])
    wot = dscr("wot", [16, 128, KC * 128])
    kt_p = [dscr(f"kt_p{i}", [8, 128, 8192]) for i in range(2)]
    v_p = [dscr(f"v_p{i}", [8, 128, 64 * 128]) for i in range(2)]
    kt_s = [dscr(f"kt_s{i}", [4, 8, 128, 1152]) for i in range(2)]
    v_s = [dscr(f"v_s{i}", [4, 8, 128, 9 * 128]) for i in range(2)]

    def sb(name, shape, dt):
        return es.enter_context(nc.sbuf_tensor(name, list(shape), dt))

    xT = sb("xT", [128, KC, NT], F32)
    xn = sb("xn", [128, KC, NT], BF16)
    H = sb("H", [128, 32, NT], BF16)
    y1 = sb("y1", [128, KC, NT], F32)
    NW = 3
    wsl = [sb(f"wsl{i}", [128, 6144], BF16) for i in range(NW)]
    sqt = [sb(f"sq{i}", [128, NT], BF16) for i in range(2)]
    rstd = sb("rstd", [128, NT], F32)
    lnt = sb("lnt", [128, NT], F32)
    st32 = [sb(f"st32_{i}", [128, 512], F32) for i in range(3)]
    stb = [sb(f"stb_{i}", [128, 512], BF16) for i in range(3)]
    ktst = [sb(f"ktst{i}", [128, 4, NT], BF16) for i in range(2)]
    cst = [sb(f"cos{i}", [128, 128], F32) for i in range(2)]
    snt = [sb(f"sin{i}", [128, 128], F32) for i in range(2)]
    rtmp = [sb(f"rtmp{i}", [128, 4, 64], F32) for i in range(2)]
    NKV = 4
    kring = [sb(f"kring{i}", [128, 512], BF16) for i in range(NKV)]
    vring = [sb(f"vring{i}", [128, 512], BF16) for i in range(NKV)]
    gains = sb("gains_sb", [128, 96], F32)
    subg = sb("subg_sb", [128, 1], F32)
    gsub = sb("gsub", [128, 1], F32)
    lamv = sb("lam_sb", [128, 256], F32)
    lamp = sb("lamp", [128, 64], F32)
    lams = sb("lams", [128, 4], F32)
    nlam = sb("nlam", [128, 1], F32)
    ident_f = sb("ident_f", [128, 128], F32)
    ident_b = sb("ident_b", [128, 128], BF16)
    ntri_f = sb("ntri_f", [128, 128], F32)
    ntri_b = sb("ntri_b", [128, 128], BF16)
    ones_b = sb("ones_b", [128, 128], BF16)
    nones_b = sb("nones_b", [1, 128], BF16)
    maskd = sb("maskd", [128, 2048], BF16)
    masks = sb("masks", [128, 2048], BF16)
    auxb = sb("auxb_sb", [128, 1], F32)
    zerob = sb("zerob", [128, 1], F32)
    epsb = sb("epsb", [128, 1], F32)
    y1flat = y1[:].rearrange("p a b -> p (a b)")
    y1b = y1flat.bitcast(BF16)

    def y1f(off, n):
        return y1flat[:, off:off + n]

    def y1h(off, n):
        return y1b[:, 2 * off:2 * off + n]

    banks = [es.enter_context(nc.psum_tensor(f"bank{i}", [128, 512], F32)) for i in range(8)]

    B = lambda n: Buf(n)
    b_xT = [B(f"xT{k}") for k in range(KC)]
    b_xn = [B(f"xn{k}") for k in range(KC)]
    b_H = [B(f"H{k}") for k in range(32)]
    b_y1 = [B(f"y1_{k}") for k in range(KC)]
    b_w = [B(f"w{i}") for i in range(NW)]
    b_sq = [B(f"sq{i}") for i in range(2)]
    b_rstd, b_lnt = B("rstd"), B("lnt")
    b_st32 = [B(f"st32_{i}") for i in range(3)]
    b_stb = [B(f"stb{i}") for i in range(3)]
    b_ktst = [B(f"ktst{i}") for i in range(2)]
    b_cs = [B(f"cs{i}") for i in range(2)]
    b_rtmp = [B(f"rtmp{i}") for i in range(2)]
    b_kv = [B(f"kv{i}") for i in range(NKV)]
    b_const = B("const")
    b_bank = [B(f"bank{i}") for i in range(8)]
    b_xst = [B("xst0"), B("xst1")]
    h_yst = [B("hyst0"), B("hyst1")]
    for s in range(2):
        for k in range(4):
            overlap(b_xst[s], b_y1[4 * s + k])
    WOFF = 0

    class WT:
        def __init__(self, name, off, n, dt):
            self.buf = B(name)
            self.ap = y1f(off, n) if dt == F32 else y1h(off, n)
            k0 = off // 512
            k1 = (off + (n if dt == F32 else (n + 1) // 2) - 1) // 512
            for k in range(k0, k1 + 1):
                overlap(self.buf, b_y1[k])
            for s_ in range(2):
                if k0 < 4 * s_ + 4 and k1 >= 4 * s_:
                    overlap(self.buf, b_xst[s_])

    woff = [WOFF]

    def walloc(name, n, dt):
        w = WT(name, woff[0], n, dt)
        woff[0] += n if dt == F32 else n // 2
        assert woff[0] <= 16 * 512
        return w

    w_p = [[walloc(f"p{m}{s}", 512, BF16) for s in range(2)] for m in range(2)]
    w_e = [walloc(f"e{s}", 512, F32) for s in range(2)]
    w_sp = [walloc(f"sp{s}", 512, BF16) for s in range(2)]
    w_a = [walloc(f"a{s}", 512, BF16) for s in range(2)]
    w_t = [walloc(f"t{s}", 512, F32) for s in range(3)]
    w_z0 = walloc("z0", 1024, F32)
    w_cb = walloc("cb", 1024, BF16)
    z0t, cbt = w_z0.ap, w_cb.ap
    b_z0 = [B('cprev0'), B('cprev1')]
    for _b in b_z0:
        for _o in list(w_z0.buf.overlaps):
            overlap(_b, _o)
    b_cb = [w_cb.buf, w_cb.buf]

    dbufs = {}

    def dbuf(key):
        if key not in dbufs:
            dbufs[key] = B(str(key))
        return dbufs[key]

    rr = {"w": 0, "sq": 0, "st": 0, "kv": 0, "cast": 0}

    def nxt(key, n):
        v = rr[key]
        rr[key] = (v + 1) % n
        return v

    def dma(eng, out, in_, reads, writes, owner):
        return P.op(eng, lambda e, o=out, i=in_: e.dma_start(out=o, in_=i), reads, writes, dma=owner)

    def mm(out, lhsT, rhs, start, stop, reads, writes):
        return P.op("pe", lambda e, o=out, l=lhsT, r=rhs, s=start, t=stop: e.matmul(o, l, r, start=s, stop=t), reads, writes)

    def act(out, in_, func, reads, writes, bias=None, scale=None):
        kw = {}
        if bias is not None:
            kw["bias"] = bias
        if scale is not None:
            kw["scale"] = scale
        return P.op("act", lambda e, o=out, i=in_, f=func, k=kw: e.activation(out=o, in_=i, func=f, **k), reads, writes)

    def cp(eng, out, in_, reads, writes):
        if eng == "act":
            return P.op("act", lambda e, o=out, i=in_: e.copy(out=o, in_=i), reads, writes)
        return P.op(eng, lambda e, o=out, i=in_: e.tensor_copy(out=o, in_=i), reads, writes)

    def tt(eng, out, in0, in1, op, reads, writes):
        return P.op(eng, lambda e, o=out, a=in0, b=in1, p=op: e.tensor_tensor(out=o, in0=a, in1=b, op=p), reads, writes)

    def stt(eng, out, in0, scalar, in1, op0, op1, reads, writes):
        return P.op(eng, lambda e, o=out, a=in0, s=scalar, b=in1, p0=op0, p1=op1:
                    e.scalar_tensor_tensor(out=o, in0=a, scalar=s, in1=b, op0=p0, op1=p1), reads, writes)

    def load_consts():
        for dst, src in ((gains, gains_d), (subg, subg_d), (lamv, lam_d), (ident_f, ident_d), (ntri_f, ntri_d),
                         (auxb, auxb_d)):
            dma("sp", dst[:], src[:, :], [], [b_const], b_const)
        P.op("dve", lambda e: e.memset(ones_b[:], 1.0), [], [b_const])
        P.op("dve", lambda e: e.memset(nones_b[:], -1.0), [], [b_const])
        P.op("dve", lambda e: e.memset(zerob[:], 0.0), [], [b_const])
        cp("dve", ident_b[:], ident_f[:], [b_const], [b_const])
        cp("dve", ntri_b[:], ntri_f[:], [b_const], [b_const])
        mkf = H[:].rearrange("p a b -> p (a b)").bitcast(F32)
        dma("sp", mkf[:, 0:2048], maskd_d[:, :], [], [b_const] + b_H[0:8], b_const)
        cp("dve", maskd[:], mkf[:, 0:2048], [b_const] + b_H[0:8], [b_const])
        dma("sp", mkf[:, 2048:4096], masks_d[:, :], [], [b_const] + b_H[8:16], b_const)
        cp("dve", masks[:], mkf[:, 2048:4096], [b_const] + b_H[8:16], [b_const])
        P.op("dve", lambda e: e.memset(epsb[:], EPS), [], [b_const])
        P.op("dve", lambda e: e.tensor_scalar(out=gsub[:], in0=subg[:], scalar1=1.0 - LAM_INIT, scalar2=None,
                                              op0=ALU.mult), [b_const], [b_const])
        for i in range(2):
            tt("dve", lamp[:], lamv[:, 128 * i:128 * i + 64], lamv[:, 128 * i + 64:128 * i + 128], ALU.mult,
               [b_const], [b_const])
            P.op("dve", lambda e, i=i: e.tensor_reduce(out=lams[:, i:i + 1], in_=lamp[:], axis=AX.X, op=ALU.add),
                 [b_const], [b_const])
            act(lams[:, 2 + i:3 + i], lams[:, i:i + 1], AF.Exp, [b_const], [b_const])
        tt("dve", nlam[:], lams[:, 3:4], lams[:, 2:3], ALU.subtract, [b_const], [b_const])
        P.op("dve", lambda e: e.tensor_scalar(out=nlam[:], in0=nlam[:], scalar1=-LAM_INIT, scalar2=None, op0=ALU.add),
             [b_const], [b_const])

    def convert_weights():
        Hf = H[:].rearrange("p a b -> p (a b)").bitcast(F32)
        xnf = xn[:].rearrange("p a b -> p (a b)")
        sin_ = [Hf[:, 0:4096], Hf[:, 4096:8192]]
        sout = [xnf[:, 0:4096], xnf[:, 4096:8192]]
        bin_ = [b_H[0:16], b_H[16:32]]
        bout = [b_xn[0:8], b_xn[8:16]]
        own_in = [B("cvin0"), B("cvin1")]
        own_out = [B("cvout0"), B("cvout1")]
        cnt = [0]

        def slab(src_ap_3d, dims_in, cast_in_view, cast_out_view, dst_ap, n_el, store_view=None):
            i = cnt[0] % 2
            cnt[0] += 1
            a, b_ = dims_in
            dma("sp", sin_[i][:, 0:n_el].rearrange("p (a b) -> p a b", a=a), src_ap_3d, [], bin_[i], own_in[i])
            ceng = ("dve", "pool", "act")[nxt("cast", 3)]
            cp(ceng, cast_out_view(sout[i]), cast_in_view(sin_[i]), bin_[i], bout[i])
            sv = sout[i][:, 0:n_el] if store_view is None else store_view(sout[i][:, 0:n_el])
            dma("sp", dst_ap, sv, bout[i], [dbuf("w")], own_out[i])

        def stationary(src, K, col0, ncols, dst, slot, nslot, c_base=0):
            kc = K // 128
            sw = 4096 // kc
            nch = sw // 128
            for s0 in range(0, ncols, sw):
                w = min(sw, ncols - s0)
                nc_ = w // 128
                src3 = src[:, col0 + s0:col0 + s0 + w].rearrange("(k p) n -> p k n", p=128)
                c0 = c_base + s0 // 128
                d = dst[c0:c0 + nc_, :, slot * kc * 128:(slot + 1) * kc * 128].rearrange("c p f -> p c f")
                slab(src3, (kc, w),
                     lambda t, kc=kc, w=w, nc_=nc_: t[:, 0:kc * w].rearrange("p (k c n) -> p c k n", k=kc, c=nc_),
                     lambda t, kc=kc, w=w, nc_=nc_: t[:, 0:kc * w].rearrange("p (c k n) -> p c k n", c=nc_, k=kc),
                     d, kc * w, lambda t, nc_=nc_: t.rearrange("p (c f) -> p c f", c=nc_))

        for f in (1, 2):
            g, u, dn = fw[f]
            stationary(g, D, 0, DFF, wgu[f], 0, 2)
            stationary(u, D, 0, DFF, wgu[f], 1, 2)
            for cc0 in range(0, NCF, 2):
                ncc = min(2, NCF - cc0)
                src3 = dn[cc0 * 128:(cc0 + ncc) * 128, :].rearrange("(c p) n -> p c n", p=128)
                d = wd[f][:, :, cc0 * 128:(cc0 + ncc) * 128].rearrange("o p f -> p o f")
                slab(src3, (ncc, D),
                     lambda t, ncc=ncc: t[:, 0:ncc * D].rearrange("p (c o n) -> p o c n", c=ncc, o=16),
                     lambda t, ncc=ncc: t[:, 0:ncc * D].rearrange("p (o c n) -> p o c n", o=16, c=ncc),
                     d, ncc * D, lambda t: t.rearrange("p (o f) -> p o f", o=16))
        stationary(w_in, D, 6144, 2048, wmg[:, :, 0:2048], 0, 1)
        stationary(w_in, D, 8192, 2048, wmg[:, :, 2048:4096], 0, 1)
        stationary(w_up_a, 1024, 0, 2048, wmg[:, :, 4096:5120], 0, 1)
        stationary(w_up_b, 1024, 0, 2048, wmg[:, :, 5120:6144], 0, 1)
        stationary(w_o, D, 0, 2048, wot, 0, 1)
        for cg in range(12):
            for hf in range(2):
                src3 = w_in[hf * 1024:(hf + 1) * 1024, cg * 512:(cg + 1) * 512].rearrange("(k p) n -> p k n", p=128)
                slab(src3, (8, 512), lambda t: t[:, 0:4096], lambda t: t[:, 0:4096], wqkv[cg, hf], 4096)

    def wload(src_ap, n):
        i = nxt("w", NW)
        dma("sp", wsl[i][:, 0:n], src_ap, [dbuf("w")], [b_w[i]], b_w[i])
        return i

    def rms_stats(src_ap_fn, src_bufs, nt, nfeat_chunks, dim, bank):
        for k in range(nfeat_chunks):
            s = nxt("sq", 2)
            act(sqt[s][:, 0:nt], src_ap_fn(k), AF.Square, [src_bufs[k]], [b_sq[s]])
            mm(banks[bank][:, 0:nt], ones_b[:], sqt[s][:, 0:nt], k == 0, k == nfeat_chunks - 1,
               [b_sq[s], b_const], [b_bank[bank]])
        act(lnt[:, 0:nt], banks[bank][:, 0:nt], AF.Ln, [b_bank[bank], b_const], [b_lnt], bias=epsb[:, 0:1], scale=1.0 / dim)
        act(rstd[:, 0:nt], lnt[:, 0:nt], AF.Exp, [b_lnt], [b_rstd], scale=-0.5)

    def norm_to_xn(gidx, nt):
        rms_stats(lambda k: xT[:, k, 0:nt], b_xT, nt, KC, D, 7)
        for k in range(KC):
            stt("dve", xn[:, k, 0:nt], xT[:, k, 0:nt], gains[:, gidx * 16 + k:gidx * 16 + k + 1], rstd[:, 0:nt],
                ALU.mult, ALU.mult, [b_xT[k], b_rstd, b_const], [b_xn[k]])

    def post_residual(gidx, nt, half):
        rms_stats(lambda k: y1[:, k, 0:nt], b_y1, nt, KC, D, 7)
        for k in range(KC):
            stt("dve", y1[:, k, 0:nt], y1[:, k, 0:nt], gains[:, gidx * 16 + k:gidx * 16 + k + 1], rstd[:, 0:nt],
                ALU.mult, ALU.mult, [b_y1[k], b_rstd, b_const], [b_y1[k]])
            stt("dve", xT[:, k, 0:nt], y1[:, k, 0:nt], half, xT[:, k, 0:nt], ALU.mult, ALU.add,
                [b_y1[k], b_xT[k]], [b_xT[k]])

    def ffn(f, nt):
        for hf, (c0, c1) in enumerate(((0, HALF0), (HALF0, NCF))):
            for c in range(c0, c1):
                i = wload(wgu[f][c], 4096)
                wt = wsl[i][:, 0:4096].rearrange("p (s k n) -> p s k n", s=2, k=KC)
                par = (c % 2) * 2
                for s in range(2):
                    for k in range(KC):
                        mm(banks[par + s][:, 0:nt], wt[:, s, k, :], xn[:, k, 0:nt], k == 0, k == KC - 1,
                           [b_w[i], b_xn[k]], [b_bank[par + s]])
                sgi = c % 2
                act(st32[sgi][:, 0:nt], banks[par][:, 0:nt], AF.Silu, [b_bank[par]], [b_st32[sgi]])
                tt("dve", H[:, c - c0, 0:nt], st32[sgi][:, 0:nt], banks[par + 1][:, 0:nt], ALU.mult,
                   [b_st32[sgi], b_bank[par + 1]], [b_H[c - c0]])
            for o in range(16):
                i = wload(wd[f][o, :, c0 * 128:c1 * 128], (c1 - c0) * 128)
                wt = wsl[i][:, 0:(c1 - c0) * 128].rearrange("p (c n) -> p c n", n=128)
                bk = 4 + (o % 3)
                for c in range(c0, c1):
                    mm(banks[bk][:, 0:nt], wt[:, c - c0, :], H[:, c - c0, 0:nt], c == c0, c == c1 - 1,
                       [b_w[i], b_H[c - c0]], [b_bank[bk]])
                if hf == 0:
                    cp("act", y1[:, o, 0:nt], banks[bk][:, 0:nt], [b_bank[bk]], [b_y1[o]])
                else:
                    tt("dve", y1[:, o, 0:nt], y1[:, o, 0:nt], banks[bk][:, 0:nt], ALU.add,
                       [b_y1[o], b_bank[bk]], [b_y1[o]])

    def load_x(src, row0, nt):
        nsub = nt // 128
        for sub in range(nsub):
            s = sub % 2
            xst = y1f(s * 2048, 2048)
            dma("sp", xst, src[row0 + sub * 128:row0 + (sub + 1) * 128, :], [], [b_xst[s]], b_xst[s])
            for g in range(4):
                bk = g % 4
                for j in range(4):
                    k = 4 * g + j
                    P.op("pe", lambda e, o=banks[bk][:, j * 128:(j + 1) * 128], i=xst[:, k * 128:(k + 1) * 128]:
                         e.transpose(o, i, ident_f[:]), [b_xst[s], b_const], [b_bank[bk]])
                eng = "act" if g % 2 == 0 else "dve"
                cp(eng, xT[:, 4 * g:4 * g + 4, sub * 128:(sub + 1) * 128],
                   banks[bk][:, :].rearrange("p (j t) -> p j t", j=4), [b_bank[bk]], b_xT[4 * g:4 * g + 4])

    def store_y(dst, row0, nt):
        nsub = nt // 128
        for sub in range(nsub):
            s = sub % 2
            yst = y1f(s * 2048, 2048)
            for g in range(4):
                bk = g % 4
                for j in range(4):
                    k = 4 * g + j
                    P.op("pe", lambda e, o=banks[bk][:, j * 128:(j + 1) * 128], i=xT[:, k, sub * 128:(sub + 1) * 128]:
                         e.transpose(o, i, ident_f[:]), [b_xT[k], b_const], [b_bank[bk]])
                eng = "act" if g % 2 == 0 else "dve"
                cp(eng, yst[:, g * 512:(g + 1) * 512], banks[bk][:, :], [b_bank[bk]], [b_xst[s]])
            dma("pool", dst[row0 + sub * 128:row0 + (sub + 1) * 128, :], yst, [b_xst[s]], [dbuf("yout")], h_yst[s])

    def qkv_proj(kind, nt, pos_row0, tok0, outs, out_row0, kt_dst, v_dst):
        nsub = nt // 128
        cgs = range(12) if kind != "aux" else (2, 3, 4, 5, 8, 9, 10, 11)
        QA_SCALE = 0.125
        QB_SCALE = 128.0 ** -0.5
        for cg in cgs:
            typ = cg // 2
            hh = cg % 2
            for hf in range(2):
                i = wload(wqkv[cg, hf], 4096)
                wt = wsl[i][:, 0:4096].rearrange("p (k n) -> p k n", k=8)
                for sub in range(nsub):
                    for k in range(8):
                        mm(banks[sub][:, :], xn[:, hf * 8 + k, sub * 128:(sub + 1) * 128], wt[:, k, :],
                           hf == 0 and k == 0, hf == 1 and k == 7, [b_w[i], b_xn[hf * 8 + k]], [b_bank[sub]])
            kts = nxt("kt", 2) if typ in (1, 4) else None
            for sub in range(nsub):
                s = nxt("st", 3)
                if typ == 0:
                    P.op("act", lambda e, o=st32[s][:], i=banks[sub][:, :]: e.mul(out=o, in_=i, mul=QA_SCALE),
                         [b_bank[sub]], [b_st32[s]])
                elif typ == 3:
                    P.op("act", lambda e, o=st32[s][:], i=banks[sub][:, :]: e.mul(out=o, in_=i, mul=QB_SCALE),
                         [b_bank[sub]], [b_st32[s]])
                else:
                    cp("act", st32[s][:], banks[sub][:, :], [b_bank[sub]], [b_st32[s]])
                if typ in (0, 1):
                    c = nxt("cs", 2)
                    r0 = pos_row0 + sub * 128
                    dma("pool", cst[c][:], cos_t[r0:r0 + 128, :], [], [b_cs[c]], b_cs[c])
                    dma("pool", snt[c][:], sin_t[r0:r0 + 128, :], [], [b_cs[c]], b_cs[c])
                    v = st32[s][:].rearrange("p (g d) -> p g d", d=64)
                    x1, x2 = v[:, :, 0:8], v[:, :, 8:16]
                    cv = cst[c][:].rearrange("p (g d) -> p g d", d=8)[:, 0:8, :]
                    sv = snt[c][:].rearrange("p (g d) -> p g d", d=8)[:, 0:8, :]
                    r = nxt("rt", 2)
                    t = rtmp[r][:].rearrange("p a (g d) -> p a g d", d=8)
                    rb = [b_st32[s], b_cs[c]]
                    tt("pool", t[:, 0], x1, cv, ALU.mult, rb, [b_rtmp[r]])
                    tt("pool", t[:, 1], x2, sv, ALU.mult, rb, [b_rtmp[r]])
                    tt("pool", t[:, 2], x2, cv, ALU.mult, rb, [b_rtmp[r]])
                    tt("pool", t[:, 3], x1, sv, ALU.mult, rb, [b_rtmp[r]])
                    tt("pool", x1, t[:, 0], t[:, 1], ALU.subtract, [b_rtmp[r]], [b_st32[s]])
                    tt("pool", x2, t[:, 2], t[:, 3], ALU.add, [b_rtmp[r]], [b_st32[s]])
                if typ in (1, 2, 4, 5) and kind != "aux":
                    oi = {1: 0, 2: 1, 4: 2, 5: 3}[typ]
                    r0 = out_row0 + sub * 128
                    dma("pool", outs[oi][r0:r0 + 128, hh * 512:(hh + 1) * 512], st32[s][:], [b_st32[s]],
                        [dbuf("kvout")], b_st32[s])
                cp("dve", stb[s][:], st32[s][:], [b_st32[s]], [b_stb[s]])
                if typ in (2, 5):
                    br = 0 if typ == 2 else 1
                    for (dap, p0, p1) in v_dst(br, hh, sub):
                        dma("pool", dap, stb[s][p0:p1, :].rearrange("p (h e) -> p h e", e=128), [b_stb[s]],
                            [dbuf(("v", kind, br, tok0, hh, sub))], b_stb[s])
                else:
                    bk = 4 + (sub % 2)
                    bv = banks[bk][:, :].bitcast(BF16)
                    for j in range(4):
                        P.op("pe", lambda e, o=bv[:, j * 128:(j + 1) * 128], i=stb[s][:, j * 128:(j + 1) * 128]:
                             e.transpose(o, i, ident_b[:]), [b_stb[s], b_const], [b_bank[bk]])
                    if typ in (0, 3):
                        base = (0 if typ == 0 else 8) + hh * 4
                        cp("dve", H[:, base:base + 4, sub * 128:(sub + 1) * 128],
                           bv[:, 0:512].rearrange("p (j t) -> p j t", j=4), [b_bank[bk]], b_H[base:base + 4])
                    else:
                        cp("dve", ktst[kts][:, :, sub * 128:(sub + 1) * 128],
                           bv[:, 0:512].rearrange("p (j t) -> p j t", j=4), [b_bank[bk]], [b_ktst[kts]])
            if typ in (1, 4):
                br = 0 if typ == 1 else 1
                for (dap, t0, t1) in kt_dst(br, hh):
                    dma("pool", dap, ktst[kts][:, :, t0:t1], [b_ktst[kts]],
                        [dbuf(("kt", kind, br, tok0, hh))], b_ktst[kts])

    rr["kt"] = 0
    rr["cs"] = 0
    rr["rt"] = 0

    def attention(nq, groups, q0=0):
        AM = cfg.get("am", "")
        for h in (range(8) if AM == "" else (range(1) if AM.startswith("diff") else range(0))):
            ntile_total = sum((g["nk"] + 127) // 128 for g in groups)
            ti = 0
            for g in groups:
                sl = nxt("kv", NKV)
                nk = g["nk"]
                ntl = (nk + 127) // 128
                dma("pool", kring[sl][:, 0:nk], g["kt"](0, h), g["bufs"](0), [b_kv[sl]], b_kv[sl])
                vv = vring[sl][:, 0:ntl * 128]
                dma("pool", vv if ntl * 128 == nk or ntl > 1 else vring[sl][0:nk, 0:128], g["v"](0, h), g["bufs"](0),
                    [b_kv[sl]], b_kv[sl])
                for t in range(ntl):
                    n = min(128, nk - t * 128)
                    par = ti % 2
                    mk = g["mask_d"].get(t)
                    for m in range(2):
                        bk = 2 * par + m
                        mm(banks[bk][0:n, 0:nq], kring[sl][64 * m:64 * m + 64, t * 128:t * 128 + n],
                           H[64 * m:64 * m + 64, h, q0:q0 + nq], True, mk is None, [b_kv[sl], b_H[h]], [b_bank[bk]])
                        if mk is not None:
                            mm(banks[bk][0:n, 0:nq], ident_b[0:n, 0:n], maskd[0:n, mk:mk + nq], False, True,
                               [b_const], [b_bank[bk]])
                    for m in range(2):
                        bk = 2 * par + m
                        pw = w_p[m][par]
                        act(pw.ap[0:n, 0:nq], banks[bk][0:n, 0:nq], AF.Exp, [b_bank[bk], b_const], [pw.buf],
                            bias=g["bias"][0:n, :])
                    for m in range(2):
                        pw = w_p[m][par]
                        mm(banks[4 + m][:, 0:nq], vring[sl][0:n, t * 128:(t + 1) * 128], pw.ap[0:n, 0:nq],
                           ti == 0, ti == ntile_total - 1, [b_kv[sl], pw.buf], [b_bank[4 + m]])
                        mm(banks[6 + m][:, 0:nq], ones_b[0:n, :], pw.ap[0:n, 0:nq],
                           ti == 0, ti == ntile_total - 1, [b_const, pw.buf], [b_bank[6 + m]])
                    ti += 1
            if AM == "diff_noepi":
                continue
            t1, t2 = w_t[0], w_t[1]
            P.op("dve", lambda e, o=t1.ap[:, 0:nq], i=banks[6][:, 0:nq]: e.reciprocal(out=o, in_=i), [b_bank[6]], [t1.buf])
            tt("dve", t1.ap[:, 0:nq], t1.ap[:, 0:nq], banks[4][:, 0:nq], ALU.mult, [t1.buf, b_bank[4]], [t1.buf])
            P.op("dve", lambda e, o=t2.ap[:, 0:nq], i=banks[7][:, 0:nq]: e.reciprocal(out=o, in_=i), [b_bank[7]], [t2.buf])
            tt("dve", t2.ap[:, 0:nq], t2.ap[:, 0:nq], banks[5][:, 0:nq], ALU.mult, [t2.buf, b_bank[5]], [t2.buf])
            stt("dve", t1.ap[:, 0:nq], t2.ap[:, 0:nq], nlam[:, 0:1], t1.ap[:, 0:nq], ALU.mult, ALU.add,
                [t1.buf, t2.buf, b_const], [t1.buf])
            s = nxt("sq", 2)
            act(sqt[s][:, 0:nq], t1.ap[:, 0:nq], AF.Square, [t1.buf], [b_sq[s]])
            mm(banks[0][:, 0:nq], ones_b[:], sqt[s][:, 0:nq], True, True, [b_sq[s], b_const], [b_bank[0]])
            act(t2.ap[:, 0:nq], banks[0][:, 0:nq], AF.Ln, [b_bank[0], b_const], [t2.buf], bias=epsb[:, 0:1], scale=1.0 / 128)
            act(t2.ap[:, 0:nq], t2.ap[:, 0:nq], AF.Exp, [t2.buf], [t2.buf], scale=-0.5)
            stt("dve", H[:, 16 + h, q0:q0 + nq], t1.ap[:, 0:nq], gsub[:, 0:1], t2.ap[:, 0:nq], ALU.mult, ALU.mult,
                [t1.buf, t2.buf, b_const], [b_H[16 + h]])
        for h in (range(8) if AM == "" else (range(1) if AM.startswith("sb") else range(0))):
            hs = h % 2
            ob = 4 + hs
            cbk = 6 + hs
            cprev = z0t[:, hs * 512:hs * 512 + nq]
            ntile_total = sum((g["nk"] + 127) // 128 for g in groups)
            ti = 0
            for g in reversed(groups):
                sl = nxt("kv", NKV)
                nk = g["nk"]
                ntl = (nk + 127) // 128
                dma("pool", kring[sl][:, 0:nk], g["kt"](1, h), g["bufs"](1), [b_kv[sl]], b_kv[sl])
                dma("pool", vring[sl][:, 0:ntl * 128] if (ntl * 128 == nk or ntl > 1) else vring[sl][0:nk, 0:128],
                    g["v"](1, h), g["bufs"](1), [b_kv[sl]], b_kv[sl])
                for t in reversed(range(ntl)):
                    n = min(128, nk - t * 128)
                    par = ti % 2
                    zb = 2 * hs + par
                    mk = g["mask_s"].get(t)
                    first, last = ti == 0, ti == ntile_total - 1
                    mm(banks[zb][0:n, 0:nq], kring[sl][:, t * 128:t * 128 + n], H[:, 8 + h, q0:q0 + nq], True, False,
                       [b_kv[sl], b_H[8 + h]], [b_bank[zb]])
                    if mk is not None:
                        mm(banks[zb][0:n, 0:nq], ident_b[0:n, 0:n], masks[0:n, mk:mk + nq], False, False,
                           [b_const], [b_bank[zb]])
                    ew, spw, aw, tw = w_e[par], w_sp[par], w_a[par], w_t[par]
                    act(ew.ap[0:n, 0:nq], banks[zb][0:n, 0:nq], AF.Exp, [b_bank[zb], b_const], [ew.buf],
                        bias=g["bias"][0:n, :])
                    act(spw.ap[0:n, 0:nq], ew.ap[0:n, 0:nq], AF.Ln, [ew.buf], [spw.buf], bias=1.0, scale=1.0)
                    mm(banks[zb][0:n, 0:nq], ntri_b[0:n, 0:n], spw.ap[0:n, 0:nq], False, True,
                       [spw.buf, b_const], [b_bank[zb]])
                    if first:
                        act(aw.ap[0:n, 0:nq], banks[zb][0:n, 0:nq], AF.Exp, [b_bank[zb], b_const], [aw.buf],
                            bias=g["bias"][0:n, :])
                    else:
                        tt("dve", tw.ap[0:n, 0:nq], banks[zb][0:n, 0:nq], cprev[0:n, :], ALU.subtract,
                           [b_bank[zb], b_z0[hs]], [tw.buf])
                        act(aw.ap[0:n, 0:nq], tw.ap[0:n, 0:nq], AF.Exp, [tw.buf, b_const], [aw.buf],
                            bias=g["bias"][0:n, :])
                    if not last:
                        mm(banks[cbk][:, 0:nq], ones_b[0:n, :], spw.ap[0:n, 0:nq], first, ti == ntile_total - 2,
                           [spw.buf, b_const], [b_bank[cbk]])
                        cp("dve", cprev, banks[cbk][:, 0:nq], [b_bank[cbk]], [b_z0[hs]])
                    mm(banks[ob][:, 0:nq], vring[sl][0:n, t * 128:(t + 1) * 128], aw.ap[0:n, 0:nq],
                       first, last, [b_kv[sl], aw.buf], [b_bank[ob]])
                    ti += 1
            cp("act", H[:, 24 + h, q0:q0 + nq], banks[ob][:, 0:nq], [b_bank[ob]], [b_H[24 + h]])

    def merge_wo(nt):
        for o in range(16):
            i = wload(wmg[o], 6144)
            wt = wsl[i][:, 0:6144].rearrange("p (k n) -> p k n", n=128)
            par = 4 * (o % 2)
            for k in range(8):
                mm(banks[par][:, 0:nt], wt[:, 32 + k, :], H[:, 16 + k, 0:nt], k == 0, k == 7,
                   [b_w[i], b_H[16 + k]], [b_bank[par]])
            for k in range(KC):
                mm(banks[par + 1][:, 0:nt], wt[:, k, :], xn[:, k, 0:nt], k == 0, k == KC - 1,
                   [b_w[i], b_xn[k]], [b_bank[par + 1]])
            for k in range(8):
                mm(banks[par + 2][:, 0:nt], wt[:, 40 + k, :], H[:, 24 + k, 0:nt], k == 0, k == 7,
                   [b_w[i], b_H[24 + k]], [b_bank[par + 2]])
            for k in range(KC):
                mm(banks[par + 3][:, 0:nt], wt[:, 16 + k, :], xn[:, k, 0:nt], k == 0, k == KC - 1,
                   [b_w[i], b_xn[k]], [b_bank[par + 3]])
            sa, sbw = w_t[0], w_t[1]
            act(sa.ap[:, 0:nt], banks[par + 1][:, 0:nt], AF.Sigmoid, [b_bank[par + 1]], [sa.buf])
            act(sbw.ap[:, 0:nt], banks[par + 3][:, 0:nt], AF.Sigmoid, [b_bank[par + 3]], [sbw.buf])
            tt("dve", sa.ap[:, 0:nt], sa.ap[:, 0:nt], banks[par][:, 0:nt], ALU.mult, [sa.buf, b_bank[par]], [sa.buf])
            tt("dve", sbw.ap[:, 0:nt], sbw.ap[:, 0:nt], banks[par + 2][:, 0:nt], ALU.mult, [sbw.buf, b_bank[par + 2]],
               [sbw.buf])
            tt("dve", H[:, o, 0:nt], sa.ap[:, 0:nt], sbw.ap[:, 0:nt], ALU.add, [sa.buf, sbw.buf], [b_H[o]])
        for o in range(16):
            i = wload(wot[o], 2048)
            wt = wsl[i][:, 0:2048].rearrange("p (k n) -> p k n", n=128)
            bk = o % 4
            for k in range(KC):
                mm(banks[bk][:, 0:nt], wt[:, k, :], H[:, k, 0:nt], k == 0, k == KC - 1, [b_w[i], b_H[k]], [b_bank[bk]])
            cp("act", y1[:, o, 0:nt], banks[bk][:, 0:nt], [b_bank[bk]], [b_y1[o]])

    def convert_cache():
        for sq_ in range(4):
            for br, (ck, cv) in enumerate(((ck_a, cv_a), (ck_b, cv_b))):
                for t in range(8):
                    for isv, src in ((0, ck), (1, cv)):
                        s = nxt("st", 3)
                        for hh in range(2):
                            s = nxt("st", 3)
                            dma("pool", st32[s][:], src[sq_, t * 128:(t + 1) * 128, hh * 512:(hh + 1) * 512], [],
                                [b_st32[s]], b_st32[s])
                            cp("dve", stb[s][:], st32[s][:], [b_st32[s]], [b_stb[s]])
                            if isv:
                                dma("pool", v_s[br][sq_, 4 * hh:4 * hh + 4, :, t * 128:(t + 1) * 128]
                                    .rearrange("h p e -> p h e"),
                                    stb[s][:].rearrange("p (h e) -> p h e", e=128), [b_stb[s]],
                                    [dbuf(("cv", sq_, br))], b_stb[s])
                            else:
                                bk = 4 + (hh % 2)
                                bv = banks[bk][:, :].bitcast(BF16)
                                for j in range(4):
                                    P.op("pe", lambda e, o=bv[:, j * 128:(j + 1) * 128],
                                         i=stb[s][:, j * 128:(j + 1) * 128]: e.transpose(o, i, ident_b[:]),
                                         [b_stb[s], b_const], [b_bank[bk]])
                                kts = nxt("kt", 2)
                                cp("dve", ktst[kts][:, :, 0:128], bv[:, 0:512].rearrange("p (j t) -> p j t", j=4),
                                   [b_bank[bk]], [b_ktst[kts]])
                                dma("pool", kt_s[br][sq_, 4 * hh:4 * hh + 4, :, t * 128:(t + 1) * 128]
                                    .rearrange("h p t -> p h t"), ktst[kts][:, :, 0:128], [b_ktst[kts]],
                                    [dbuf(("ck", sq_, br))], b_ktst[kts])

    def prompt_kt_dst(tok0):
        return lambda br, hh: [(kt_p[br][4 * hh:4 * hh + 4, :, tok0:tok0 + NT].rearrange("h p t -> p h t"), 0, NT)]

    def prompt_v_dst(tok0):
        def f(br, hh, sub):
            tile = tok0 // 128 + sub
            return [(v_p[br][4 * hh:4 * hh + 4, :, tile * 128:(tile + 1) * 128].rearrange("h p e -> p h e"), 0, 128)]
        return f

    def front(kind, src, row0, nt, pos_row0, tok0, outs, out_row0, kt_dst, v_dst):
        load_x(src, row0, nt)
        norm_to_xn(0, nt)
        ffn(1, nt)
        post_residual(1, nt, 0.5)
        norm_to_xn(2, nt)
        qkv_proj(kind, nt, pos_row0, tok0, outs, out_row0, kt_dst, v_dst)

    def back(dst, row0, nt):
        merge_wo(nt)
        post_residual(3, nt, 1.0)
        norm_to_xn(4, nt)
        ffn(2, nt)
        post_residual(5, nt, 0.5)
        store_y(dst, row0, nt)

    STOP = cfg.get("stop", "")
    load_consts()
    if STOP == "consts":
        return nc, P, es
    convert_weights()
    if STOP == "weights":
        return nc, P, es
    if DO_SMP:
        convert_cache()
    for b in range(NB_AUX):
        front("aux", xa, b * NT, NT, b * NT, b * NT, None, 0, prompt_kt_dst(b * NT), prompt_v_dst(b * NT))
    if STOP == "front_aux":
        return nc, P, es
    for b in range(NB_OWN):
        tok0 = 4096 + b * NT
        front("own", xo, b * NT, NT, 4096 + b * NT, tok0, kv_o, b * NT, prompt_kt_dst(tok0), prompt_v_dst(tok0))
        if STOP == "front_own":
            return nc, P, es
        groups = []
        for gi in list(range(NB_AUX)) + [8 + j for j in range(b + 1)]:
            own = gi >= 8
            k0 = gi * 512
            keyb = (lambda br, gi=gi, own=own: [dbuf(("kt", "own" if own else "aux", br, gi * 512, hh)) for hh in range(2)]
                    + [dbuf(("v", "own" if own else "aux", br, gi * 512, hh, sub)) for hh in range(2) for sub in range(4)])
            groups.append(dict(
                nk=512,
                kt=lambda br, h, k0=k0: kt_p[br][h, :, k0:k0 + 512],
                v=lambda br, h, k0=k0: v_p[br][h, :, (k0 // 128) * 128:(k0 // 128 + 4) * 128],
                bufs=keyb,
                bias=zerob if own else auxb,
                mask_d=({t: t * 512 for t in range(4)} if gi == 8 + b else {}),
                mask_s=({t: t * 512 for t in range(4)} if gi == 8 + b else {}),
            ))
        attention(NT, groups)
        if STOP == "attn":
            return nc, P, es
        back(y_o, b * NT, NT)
    if DO_SMP:
        nt = 256

        def s_kt_dst(br, hh):
            return [(kt_s[br][i, 4 * hh:4 * hh + 4, :, 1024:1088].rearrange("h p t -> p h t"), 64 * i, 64 * i + 64)
                    for i in range(4)]

        def s_v_dst(br, hh, sub):
            return [(v_s[br][2 * sub + j, 4 * hh:4 * hh + 4, 0:64, 1024:1152].rearrange("h p e -> p h e"),
                     64 * j, 64 * j + 64) for j in range(2)]

        front("smp", xs, 0, nt, 8192, 0, kv_s, 0, s_kt_dst, s_v_dst)
        for i in range(4):
            groups = []
            for gi in range(3):
                nk = 512 if gi < 2 else 64
                k0 = gi * 512
                if gi < 2:
                    keyb = lambda br, i=i: [dbuf(("ck", i, br)), dbuf(("cv", i, br))]
                else:
                    keyb = lambda br: ([dbuf(("kt", "smp", br, 0, hh)) for hh in range(2)]
                                       + [dbuf(("v", "smp", br, 0, hh, sub)) for hh in range(2) for sub in range(2)])
                groups.append(dict(
                    nk=nk,
                    kt=lambda br, h, i=i, k0=k0, nk=nk: kt_s[br][i, h, :, k0:k0 + nk],
                    v=(lambda br, h, i=i, k0=k0: v_s[br][i, h, :, k0:k0 + 512]) if gi < 2 else
                      (lambda br, h, i=i: v_s[br][i, h, 0:64, 1024:1152]),
                    bufs=keyb, bias=zerob, mask_d={}, mask_s=({0: 0} if gi == 2 else {}),
                ))
            attention(64, groups, q0=64 * i)
        back(y_s, 0, nt)

    return nc, P, es


def emit_program(nc, P, es):
    P.finalize()
    with es:
        sems = {e: es.enter_context(nc.semaphore(f"s_{e}")) for e in ("pe", "act", "dve", "pool")}
        for i, b in enumerate(P.dma_bufs):
            b.sem = es.enter_context(nc.semaphore(f"d{i}"))
        block = es.enter_context(nc.Block())

        def run(name, e):
            waited = {}
            for op in P.ops[name]:
                need = {}
                for d in op.deps:
                    if d.dma is not None:
                        key, sem, val = id(d.dma), d.dma.sem, d.val
                    else:
                        if d.eng == name and name == "pe":
                            continue
                        key, sem, val = d.eng, sems[d.eng], d.val
                    if key not in need or need[key][1] < val:
                        need[key] = (sem, val)
                for key, (sem, val) in need.items():
                    if waited.get(key, 0) < val:
                        e.wait_ge(sem, val)
                        waited[key] = val
                ins = op.fn(e)
                if op.dma is not None:
                    ins.then_inc(op.dma.sem, 16)
                elif op.mark:
                    ins.then_inc(sems[name], 1)
            if name == "sp":
                for b in P.dma_bufs:
                    e.wait_ge(b.sem, 16 * b.ndma)

        @block.sync
        def _(e):
            run("sp", e)

        @block.gpsimd
        def _(e):
            run("pool", e)

        @block.tensor
        def _(e):
            run("pe", e)

        @block.scalar
        def _(e):
            run("act", e)

        @block.vector
        def _(e):
            run("dve", e)
    return nc


def _rope_tables(pos):
    half = 8
    inv_freq = np.power(np.float32(500000.0), -(np.arange(0, 16, 2, dtype=np.float32) / np.float32(16.0))).astype(np.float32)
    ang = pos.astype(np.float32)[:, None] * inv_freq[None, :]
    c = np.cos(ang).astype(np.float32)
    s_ = np.sin(ang).astype(np.float32)
    c = np.ascontiguousarray(np.broadcast_to(c[:, None, :], (pos.shape[0], 16, 8))).reshape(pos.shape[0], 128)
    s_ = np.ascontiguousarray(np.broadcast_to(s_[:, None, :], (pos.shape[0], 16, 8))).reshape(pos.shape[0], 128)
    return c, s_


def _consts():
    ident = np.eye(128, dtype=np.float32)
    j = np.arange(128)[:, None]
    sidx = np.arange(128)[None, :]
    ntri = np.where(j >= sidx, -1.0, 0.0).astype(np.float32)
    k = np.arange(128)[:, None]
    q = np.arange(512)[None, :]
    md = np.zeros((128, 4, 512), np.float32)
    ms = np.zeros((128, 4, 512), np.float32)
    for t in range(4):
        kk = 128 * t + k
        md[:, t, :] = np.where((kk // 64) <= (q // 64), 0.0, NEG_D)
        ms[:, t, :] = np.where(kk < q, 0.0, NEG_S)
    return ident, ntri, md.reshape(128, 2048), ms.reshape(128, 2048)


def make_in_maps(inp):
    f32 = np.float32
    g = lambda k: np.asarray(inp[k], dtype=f32)
    xp, xsm = g("x_prompt"), g("x_sample")
    ident, ntri, md, ms = _consts()
    gains = np.stack([g(n)[0].reshape(16, 128).T for n in (
        "norm_ffn1_pre", "norm_ffn1_post", "norm_mix_pre", "norm_mix_post", "norm_ffn2_pre", "norm_ffn2_post")], axis=1)
    gains = np.ascontiguousarray(gains.reshape(128, 96))
    subg = np.ascontiguousarray(g("subln_g")[0].reshape(128, 1))
    lamv = np.concatenate([g("lam_q1")[0], g("lam_k1")[0], g("lam_q2")[0], g("lam_k2")[0]])[None, :]
    lamv = np.ascontiguousarray(np.broadcast_to(lamv, (128, 256)))
    shared = dict(
        w_in=g("w_in")[0], w_up_a=g("w_up_a")[0], w_up_b=g("w_up_b")[0], w_o=g("w_o")[0],
        f1g=g("ffn1_w_gate")[0], f1u=g("ffn1_w_up")[0], f1d=g("ffn1_w_down")[0],
        f2g=g("ffn2_w_gate")[0], f2u=g("ffn2_w_up")[0], f2d=g("ffn2_w_down")[0],
        gains=gains, subg=subg, lamv=lamv, ident=ident, ntri=ntri, mask_d=md, mask_s=ms,
    )
    cka, cva, ckb, cvb = g("cache_diff_k")[0], g("cache_diff_v")[0], g("cache_sb_k")[0], g("cache_sb_v")[0]
    pos_s = (1024 + np.arange(64)).astype(np.int64)
    maps = []
    for c in range(8):
        seq, half = c // 2, c % 2
        pos = np.concatenate([np.arange(4096), half * 4096 + np.arange(4096), np.tile(pos_s, 4)])
        ct, st_ = _rope_tables(pos)
        m = dict(shared)
        m.update(
            xa=np.ascontiguousarray(xp[seq, 0:4096]),
            xo=np.ascontiguousarray(xp[seq, half * 4096:(half + 1) * 4096]),
            xs=np.ascontiguousarray(xsm[4 * c:4 * c + 4].reshape(256, D)),
            cos_t=ct, sin_t=st_,
            ck_a=np.ascontiguousarray(cka[4 * c:4 * c + 4].reshape(4, 1024, 1024)),
            cv_a=np.ascontiguousarray(cva[4 * c:4 * c + 4].reshape(4, 1024, 1024)),
            ck_b=np.ascontiguousarray(ckb[4 * c:4 * c + 4].reshape(4, 1024, 1024)),
            cv_b=np.ascontiguousarray(cvb[4 * c:4 * c + 4].reshape(4, 1024, 1024)),
            auxb=np.full((128, 1), 0.0 if half == 1 else NEG_D, f32),
        )
        maps.append(m)
    return maps


_PROG_CACHE = {}


def run_cores(inp, cfg=None):
    cfg = dict(CFG) if cfg is None else cfg
    nc, P, es = build_program(cfg)
    emit_program(nc, P, es)
    maps = make_in_maps(inp)
    n = cfg.get("ncores", 8)
    res = run_bass_kernel_spmd(nc, maps[:n], core_ids=list(range(n)))
    return res.results


def kernel(**inp):
    r = run_cores(inp)
    f32 = np.float32
    y_p = np.empty((4, 8192, D), f32)
    y_s = np.empty((32, 64, D), f32)
    pk = [np.empty((1, 4, 8192, 1024), f32) for _ in range(4)]
    sk = [np.empty((1, 32, 64, 1024), f32) for _ in range(4)]
    for c in range(8):
        seq, half = c // 2, c % 2
        sl = slice(half * 4096, (half + 1) * 4096)
        y_p[seq, sl] = r[c]["y_o"]
        y_s[4 * c:4 * c + 4] = np.asarray(r[c]["y_s"]).reshape(4, 64, D)
        for i, n in enumerate(("ka", "va", "kb", "vb")):
            pk[i][0, seq, sl] = r[c][n + "_o"]
            sk[i][0, 4 * c:4 * c + 4] = np.asarray(r[c][n + "_s"]).reshape(4, 64, 1024)
    return (y_p, y_s,
            pk[0].reshape(1, 4, 8192, 8, 2, 64), pk[1].reshape(1, 4, 8192, 8, 128),
            pk[2].reshape(1, 4, 8192, 8, 128), pk[3].reshape(1, 4, 8192, 8, 128),
            sk[0].reshape(1, 32, 64, 8, 2, 64), sk[1].reshape(1, 32, 64, 8, 128),
            sk[2].reshape(1, 32, 64, 8, 128), sk[3].reshape(1, 32, 64, 8, 128))
```

```python
import math
from contextlib import ExitStack

import numpy as np
import concourse.bass as bass
import concourse.mybir as mybir
from concourse.bass_utils import run_bass_kernel_spmd

F32 = mybir.dt.float32
BF16 = mybir.dt.bfloat16
AF = mybir.ActivationFunctionType
ALU = mybir.AluOpType
AX = mybir.AxisListType

D = 2048
KC = 16
DFF = 5504
NCF = 43
HALF0 = 22
NT = 512
EPS = 1e-6
LAM_INIT = 0.8 - 0.6 * math.exp(0.0)
NEG_D = -30000.0
NEG_S = -1024.0

CFG = dict(nb_own=8, nb_aux=8, do_smp=True)


class Buf:
    __slots__ = ("name", "writers", "readers", "overlaps", "sem", "ndma", "queue")

    def __init__(self, name):
        self.name = name
        self.writers = []
        self.readers = []
        self.overlaps = []
        self.sem = None
        self.ndma = 0
        self.queue = None


def overlap(a, b):
    a.overlaps.append(b)
    b.overlaps.append(a)


class Op:
    __slots__ = ("eng", "fn", "deps", "dma", "mark", "val")

    def __init__(self, eng, fn, deps, dma):
        self.eng = eng
        self.fn = fn
        self.deps = deps
        self.dma = dma
        self.mark = False
        self.val = 0


class Prog:
    def __init__(self):
        self.ops = {e: [] for e in ("pe", "act", "dve", "pool", "sp")}
        self.dma_bufs = []

    def op(self, eng, fn, reads=(), writes=(), dma=None):
        deps = set()
        for b in reads:
            deps.update(b.writers)
            for o in b.overlaps:
                deps.update(o.writers)
        for b in writes:
            deps.update(b.writers)
            deps.update(b.readers)
            for o in b.overlaps:
                deps.update(o.writers)
                deps.update(o.readers)
        op = Op(eng, fn, deps, dma)
        for b in reads:
            b.readers.append(op)
        for b in writes:
            b.writers = [op]
            b.readers = []
        if dma is not None:
            if dma.queue is None:
                dma.queue = eng
                self.dma_bufs.append(dma)
            assert dma.queue == eng, (dma.name, dma.queue, eng)
            dma.ndma += 1
            op.val = 16 * dma.ndma
        self.ops[eng].append(op)
        return op

    def finalize(self):
        for e, lst in self.ops.items():
            for op in lst:
                for d in op.deps:
                    d.mark = True
        for e, lst in self.ops.items():
            n = 0
            for op in lst:
                if op.dma is None and op.mark:
                    n += 1
                    op.val = n


def build_program(cfg):
    NB_OWN, NB_AUX, DO_SMP = cfg["nb_own"], cfg["nb_aux"], cfg["do_smp"]
    nc = bass.Bass("TRN2", target_bir_lowering=False)
    P = Prog()
    es = ExitStack()

    def din(name, shape, dt=F32):
        return nc.dram_tensor(name, list(shape), dt, kind="ExternalInput").ap()

    def dout(name, shape, dt=F32):
        return nc.dram_tensor(name, list(shape), dt, kind="ExternalOutput").ap()

    def dscr(name, shape, dt=BF16):
        return nc.dram_tensor(name, list(shape), dt).ap()

    xa = din("xa", [4096, D])
    xo = din("xo", [4096, D])
    xs = din("xs", [256, D])
    cos_t = din("cos_t", [8448, 128])
    sin_t = din("sin_t", [8448, 128])
    ck_a = din("ck_a", [4, 1024, 1024])
    cv_a = din("cv_a", [4, 1024, 1024])
    ck_b = din("ck_b", [4, 1024, 1024])
    cv_b = din("cv_b", [4, 1024, 1024])
    w_in = din("w_in", [D, 10240])
    w_up_a = din("w_up_a", [1024, D])
    w_up_b = din("w_up_b", [1024, D])
    w_o = din("w_o", [D, D])
    fw = {}
    for f in (1, 2):
        fw[f] = (din(f"f{f}g", [D, DFF]), din(f"f{f}u", [D, DFF]), din(f"f{f}d", [DFF, D]))
    gains_d = din("gains", [128, 96])
    subg_d = din("subg", [128, 1])
    lam_d = din("lamv", [128, 256])
    ident_d = din("ident", [128, 128])
    ntri_d = din("ntri", [128, 128])
    maskd_d = din("mask_d", [128, 2048])
    masks_d = din("mask_s", [128, 2048])
    auxb_d = din("auxb", [128, 1])
    y_o = dout("y_o", [4096, D])
    y_s = dout("y_s", [256, D])
    kv_o = [dout(n, [4096, 1024]) for n in ("ka_o", "va_o", "kb_o", "vb_o")]
    kv_s = [dout(n, [256, 1024]) for n in ("ka_s", "va_s", "kb_s", "vb_s")]
    wgu = {f: dscr(f"wgu{f}", [NCF, 128, 2 * KC * 128]) for f in (1, 2)}
    wd = {f: dscr(f"wd{f}", [16, 128, NCF * 128]) for f in (1, 2)}
    wqkv = dscr("wqkv", [12, 2, 128, 8 * 512])
    wmg = dscr("wmg", [16, 128, 48 * 128])
    wot = dscr("wot", [16, 128, KC * 128])
    kt_p = [dscr(f"kt_p{i}", [8, 128, 8192]) for i in range(2)]
    v_p = [dscr(f"v_p{i}", [8, 128, 64 * 128]) for i in range(2)]
    kt_s = [dscr(f"kt_s{i}", [4, 8, 128, 1152]) for i in range(2)]
    v_s = [dscr(f"v_s{i}", [4, 8, 128, 9 * 128]) for i in range(2)]

    def sb(name, shape, dt):
        return es.enter_context(nc.sbuf_tensor(name, list(shape), dt))

    xT = sb("xT", [128, KC, NT], F32)
    xn = sb("xn", [128, KC, NT], BF16)
    H = sb("H", [128, 32, NT], BF16)
    y1 = sb("y1", [128, KC, NT], F32)
    NW = 3
    wsl = [sb(f"wsl{i}", [128, 6144], BF16) for i in range(NW)]
    sqt = [sb(f"sq{i}", [128, NT], BF16) for i in range(2)]
    rstd = sb("rstd", [128, NT], F32)
    lnt = sb("lnt", [128, NT], F32)
    st32 = [sb(f"st32_{i}", [128, 512], F32) for i in range(3)]
    stb = [sb(f"stb_{i}", [128, 512], BF16) for i in range(3)]
    ktst = [sb(f"ktst{i}", [128, 4, NT], BF16) for i in range(2)]
    cst = [sb(f"cos{i}", [128, 128], F32) for i in range(2)]
    snt = [sb(f"sin{i}", [128, 128], F32) for i in range(2)]
    rtmp = [sb(f"rtmp{i}", [128, 4, 64], F32) for i in range(2)]
    NKV = 6
    kring = [sb(f"kring{i}", [128, 512], BF16) for i in range(NKV)]
    vring = [sb(f"vring{i}", [128, 512], BF16) for i in range(NKV)]
    gains = sb("gains_sb", [128, 96], F32)
    subg = sb("subg_sb", [128, 1], F32)
    gsub = sb("gsub", [128, 1], F32)
    lamv = sb("lam_sb", [128, 256], F32)
    lamp = sb("lamp", [128, 64], F32)
    lams = sb("lams", [128, 4], F32)
    nlam = sb("nlam", [128, 1], F32)
    ident_f = sb("ident_f", [128, 128], F32)
    ident_b = sb("ident_b", [128, 128], BF16)
    ntri_f = sb("ntri_f", [128, 128], F32)
    ntri_b = sb("ntri_b", [128, 128], BF16)
    ones_b = sb("ones_b", [128, 128], BF16)
    nones_b = sb("nones_b", [1, 128], BF16)
    maskd = sb("maskd", [128, 2048], BF16)
    masks = sb("masks", [128, 2048], BF16)
    auxb = sb("auxb_sb", [128, 1], F32)
    zerob = sb("zerob", [128, 1], F32)
    epsb = sb("epsb", [128, 1], F32)
    y1flat = y1[:].rearrange("p a b -> p (a b)")
    y1b = y1flat.bitcast(BF16)

    def y1f(off, n):
        return y1flat[:, off:off + n]

    def y1h(off, n):
        return y1b[:, 2 * off:2 * off + n]

    banks = [es.enter_context(nc.psum_tensor(f"bank{i}", [128, 512], F32)) for i in range(8)]

    B = lambda n: Buf(n)
    b_xT = [B(f"xT{k}") for k in range(KC)]
    b_xn = [B(f"xn{k}") for k in range(KC)]
    b_H = [B(f"H{k}") for k in range(32)]
    b_y1 = [B(f"y1_{k}") for k in range(KC)]
    b_w = [B(f"w{i}") for i in range(NW)]
    b_sq = [B(f"sq{i}") for i in range(2)]
    b_rstd, b_lnt = B("rstd"), B("lnt")
    b_st32 = [B(f"st32_{i}") for i in range(3)]
    b_stb = [B(f"stb{i}") for i in range(3)]
    b_ktst = [B(f"ktst{i}") for i in range(2)]
    b_cs = [B(f"cs{i}") for i in range(2)]
    b_rtmp = [B(f"rtmp{i}") for i in range(2)]
    b_kv = [B(f"kv{i}") for i in range(NKV)]
    b_const = B("const")
    b_bank = [B(f"bank{i}") for i in range(8)]
    b_xst = [B("xst0"), B("xst1")]
    h_yst = [B("hyst0"), B("hyst1")]
    for s in range(2):
        for k in range(4):
            overlap(b_xst[s], b_y1[4 * s + k])
    WOFF = 0

    class WT:
        def __init__(self, name, off, n, dt):
            self.buf = B(name)
            self.ap = y1f(off, n) if dt == F32 else y1h(off, n)
            k0 = off // 512
            k1 = (off + (n if dt == F32 else (n + 1) // 2) - 1) // 512
            for k in range(k0, k1 + 1):
                overlap(self.buf, b_y1[k])
            for s_ in range(2):
                if k0 < 4 * s_ + 4 and k1 >= 4 * s_:
                    overlap(self.buf, b_xst[s_])

    woff = [WOFF]

    def walloc(name, n, dt):
        w = WT(name, woff[0], n, dt)
        woff[0] += n if dt == F32 else n // 2
        assert woff[0] <= 16 * 512
        return w

    w_p = [[walloc(f"p{m}{s}", 512, BF16) for s in range(2)] for m in range(2)]
    w_e = [walloc(f"e{s}", 512, F32) for s in range(2)]
    w_sp = [[walloc(f"sp{s}{q}", 512, BF16) for q in range(2)] for s in range(2)]
    w_a = [[walloc(f"a{s}{q}", 512, BF16) for q in range(2)] for s in range(2)]
    w_tm = [[walloc(f"tm{s}{q}", 512, F32) for q in range(2)] for s in range(2)]
    w_t = [w_tm[0][0], w_tm[0][1], w_tm[1][0]]
    w_z0 = walloc("z0", 1024, F32)
    z0t = w_z0.ap
    b_z0 = [B('cprev0'), B('cprev1')]
    for _b in b_z0:
        for _o in list(w_z0.buf.overlaps):
            overlap(_b, _o)

    dbufs = {}

    def dbuf(key):
        if key not in dbufs:
            dbufs[key] = B(str(key))
        return dbufs[key]

    rr = {"w": 0, "sq": 0, "st": 0, "kv": 0, "cast": 0}

    def nxt(key, n):
        v = rr[key]
        rr[key] = (v + 1) % n
        return v

    def dma(eng, out, in_, reads, writes, owner):
        return P.op(eng, lambda e, o=out, i=in_: e.dma_start(out=o, in_=i), reads, writes, dma=owner)

    def mm(out, lhsT, rhs, start, stop, reads, writes):
        return P.op("pe", lambda e, o=out, l=lhsT, r=rhs, s=start, t=stop: e.matmul(o, l, r, start=s, stop=t), reads, writes)

    def act(out, in_, func, reads, writes, bias=None, scale=None):
        kw = {}
        if bias is not None:
            kw["bias"] = bias
        if scale is not None:
            kw["scale"] = scale
        return P.op("act", lambda e, o=out, i=in_, f=func, k=kw: e.activation(out=o, in_=i, func=f, **k), reads, writes)

    def cp(eng, out, in_, reads, writes):
        if eng == "act":
            return P.op("act", lambda e, o=out, i=in_: e.copy(out=o, in_=i), reads, writes)
        return P.op(eng, lambda e, o=out, i=in_: e.tensor_copy(out=o, in_=i), reads, writes)

    def tt(eng, out, in0, in1, op, reads, writes):
        return P.op(eng, lambda e, o=out, a=in0, b=in1, p=op: e.tensor_tensor(out=o, in0=a, in1=b, op=p), reads, writes)

    def stt(eng, out, in0, scalar, in1, op0, op1, reads, writes):
        return P.op(eng, lambda e, o=out, a=in0, s=scalar, b=in1, p0=op0, p1=op1:
                    e.scalar_tensor_tensor(out=o, in0=a, scalar=s, in1=b, op0=p0, op1=p1), reads, writes)

    def load_consts():
        for dst, src in ((gains, gains_d), (subg, subg_d), (lamv, lam_d), (ident_f, ident_d), (ntri_f, ntri_d),
                         (auxb, auxb_d)):
            dma("sp", dst[:], src[:, :], [], [b_const], b_const)
        P.op("dve", lambda e: e.memset(ones_b[:], 1.0), [], [b_const])
        P.op("dve", lambda e: e.memset(nones_b[:], -1.0), [], [b_const])
        P.op("dve", lambda e: e.memset(zerob[:], 0.0), [], [b_const])
        cp("dve", ident_b[:], ident_f[:], [b_const], [b_const])
        cp("dve", ntri_b[:], ntri_f[:], [b_const], [b_const])
        mkf = H[:].rearrange("p a b -> p (a b)").bitcast(F32)
        dma("sp", mkf[:, 0:2048], maskd_d[:, :], [], [b_const] + b_H[0:8], b_const)
        cp("dve", maskd[:], mkf[:, 0:2048], [b_const] + b_H[0:8], [b_const])
        dma("sp", mkf[:, 2048:4096], masks_d[:, :], [], [b_const] + b_H[8:16], b_const)
        cp("dve", masks[:], mkf[:, 2048:4096], [b_const] + b_H[8:16], [b_const])
        P.op("dve", lambda e: e.memset(epsb[:], EPS), [], [b_const])
        P.op("dve", lambda e: e.tensor_scalar(out=gsub[:], in0=subg[:], scalar1=1.0 - LAM_INIT, scalar2=None,
                                              op0=ALU.mult), [b_const], [b_const])
        for i in range(2):
            tt("dve", lamp[:], lamv[:, 128 * i:128 * i + 64], lamv[:, 128 * i + 64:128 * i + 128], ALU.mult,
               [b_const], [b_const])
            P.op("dve", lambda e, i=i: e.tensor_reduce(out=lams[:, i:i + 1], in_=lamp[:], axis=AX.X, op=ALU.add),
                 [b_const], [b_const])
            act(lams[:, 2 + i:3 + i], lams[:, i:i + 1], AF.Exp, [b_const], [b_const])
        tt("dve", nlam[:], lams[:, 3:4], lams[:, 2:3], ALU.subtract, [b_const], [b_const])
        P.op("dve", lambda e: e.tensor_scalar(out=nlam[:], in0=nlam[:], scalar1=-LAM_INIT, scalar2=None, op0=ALU.add),
             [b_const], [b_const])

    def convert_weights():
        Hf = H[:].rearrange("p a b -> p (a b)").bitcast(F32)
        xnf = xn[:].rearrange("p a b -> p (a b)")
        xTb = xT[:].rearrange("p a b -> p (a b)").bitcast(BF16)
        sin_ = [Hf[:, 0:8192], y1flat[:, 0:8192]]
        sout = [xnf[:, 0:8192], xTb[:, 0:8192]]
        bin_ = [b_H[0:32], b_y1[0:16]]
        bout = [b_xn[0:16], b_xT[0:8]]
        own_in = [B("cvin0"), B("cvin1")]
        own_out = [B("cvout0"), B("cvout1")]
        cnt = [0]

        def slab(src_ap_3d, dims_in, cast_in_view, cast_out_view, dst_ap, n_el, store_view=None):
            i = cnt[0] % 2
            cnt[0] += 1
            a, b_ = dims_in
            dma("sp", sin_[i][:, 0:n_el].rearrange("p (a b) -> p a b", a=a), src_ap_3d, [], bin_[i], own_in[i])
            ceng = ("dve", "pool", "act")[nxt("cast", 3)]
            cp(ceng, cast_out_view(sout[i]), cast_in_view(sin_[i]), bin_[i], bout[i])
            sv = sout[i][:, 0:n_el] if store_view is None else store_view(sout[i][:, 0:n_el])
            dma("sp", dst_ap, sv, bout[i], [dbuf("w")], own_out[i])

        def stationary(src, K, col0, ncols, dst, slot, nslot, c_base=0):
            kc = K // 128
            sw = 8192 // kc
            nch = sw // 128
            for s0 in range(0, ncols, sw):
                w = min(sw, ncols - s0)
                nc_ = w // 128
                src3 = src[:, col0 + s0:col0 + s0 + w].rearrange("(k p) n -> p k n", p=128)
                c0 = c_base + s0 // 128
                d = dst[c0:c0 + nc_, :, slot * kc * 128:(slot + 1) * kc * 128].rearrange("c p f -> p c f")
                slab(src3, (kc, w),
                     lambda t, kc=kc, w=w, nc_=nc_: t[:, 0:kc * w].rearrange("p (k c n) -> p c k n", k=kc, c=nc_),
                     lambda t, kc=kc, w=w, nc_=nc_: t[:, 0:kc * w].rearrange("p (c k n) -> p c k n", c=nc_, k=kc),
                     d, kc * w, lambda t, nc_=nc_: t.rearrange("p (c f) -> p c f", c=nc_))

        for f in (1, 2):
            g, u, dn = fw[f]
            stationary(g, D, 0, DFF, wgu[f], 0, 2)
            stationary(u, D, 0, DFF, wgu[f], 1, 2)
            for cc0 in range(0, NCF, 4):
                ncc = min(4, NCF - cc0)
                src3 = dn[cc0 * 128:(cc0 + ncc) * 128, :].rearrange("(c p) n -> p c n", p=128)
                d = wd[f][:, :, cc0 * 128:(cc0 + ncc) * 128].rearrange("o p f -> p o f")
                slab(src3, (ncc, D),
                     lambda t, ncc=ncc: t[:, 0:ncc * D].rearrange("p (c o n) -> p o c n", c=ncc, o=16),
                     lambda t, ncc=ncc: t[:, 0:ncc * D].rearrange("p (o c n) -> p o c n", o=16, c=ncc),
                     d, ncc * D, lambda t: t.rearrange("p (o f) -> p o f", o=16))
        stationary(w_in, D, 6144, 2048, wmg[:, :, 0:2048], 0, 1)
        stationary(w_in, D, 8192, 2048, wmg[:, :, 2048:4096], 0, 1)
        stationary(w_up_a, 1024, 0, 2048, wmg[:, :, 4096:5120], 0, 1)
        stationary(w_up_b, 1024, 0, 2048, wmg[:, :, 5120:6144], 0, 1)
        stationary(w_o, D, 0, 2048, wot, 0, 1)
        for cg in range(12):
            for hf in range(2):
                src3 = w_in[hf * 1024:(hf + 1) * 1024, cg * 512:(cg + 1) * 512].rearrange("(k p) n -> p k n", p=128)
                slab(src3, (8, 512), lambda t: t[:, 0:4096], lambda t: t[:, 0:4096], wqkv[cg, hf], 4096)

    def wload(src_ap, n):
        i = nxt("w", NW)
        dma("sp", wsl[i][:, 0:n], src_ap, [dbuf("w")], [b_w[i]], b_w[i])
        return i

    def rms_stats(src_ap_fn, src_bufs, nt, nfeat_chunks, dim, bank):
        for k in range(nfeat_chunks):
            s = nxt("sq", 2)
            act(sqt[s][:, 0:nt], src_ap_fn(k), AF.Square, [src_bufs[k]], [b_sq[s]])
            mm(banks[bank][:, 0:nt], ones_b[:], sqt[s][:, 0:nt], k == 0, k == nfeat_chunks - 1,
               [b_sq[s], b_const], [b_bank[bank]])
        act(lnt[:, 0:nt], banks[bank][:, 0:nt], AF.Ln, [b_bank[bank], b_const], [b_lnt], bias=epsb[:, 0:1], scale=1.0 / dim)
        act(rstd[:, 0:nt], lnt[:, 0:nt], AF.Exp, [b_lnt], [b_rstd], scale=-0.5)

    def norm_to_xn(gidx, nt):
        rms_stats(lambda k: xT[:, k, 0:nt], b_xT, nt, KC, D, 7)
        for k in range(KC):
            stt("dve", xn[:, k, 0:nt], xT[:, k, 0:nt], gains[:, gidx * 16 + k:gidx * 16 + k + 1], rstd[:, 0:nt],
                ALU.mult, ALU.mult, [b_xT[k], b_rstd, b_const], [b_xn[k]])

    def post_residual(gidx, nt, half):
        rms_stats(lambda k: y1[:, k, 0:nt], b_y1, nt, KC, D, 7)
        for k in range(KC):
            stt("dve", y1[:, k, 0:nt], y1[:, k, 0:nt], gains[:, gidx * 16 + k:gidx * 16 + k + 1], rstd[:, 0:nt],
                ALU.mult, ALU.mult, [b_y1[k], b_rstd, b_const], [b_y1[k]])
            stt("dve", xT[:, k, 0:nt], y1[:, k, 0:nt], half, xT[:, k, 0:nt], ALU.mult, ALU.add,
                [b_y1[k], b_xT[k]], [b_xT[k]])

    def ffn(f, nt):
        for hf, (c0, c1) in enumerate(((0, HALF0), (HALF0, NCF))):
            for c in range(c0, c1):
                i = wload(wgu[f][c], 4096)
                wt = wsl[i][:, 0:4096].rearrange("p (s k n) -> p s k n", s=2, k=KC)
                par = (c % 2) * 2
                for s in range(2):
                    for k in range(KC):
                        mm(banks[par + s][:, 0:nt], wt[:, s, k, :], xn[:, k, 0:nt], k == 0, k == KC - 1,
                           [b_w[i], b_xn[k]], [b_bank[par + s]])
                sgi = c % 2
                act(st32[sgi][:, 0:nt], banks[par][:, 0:nt], AF.Silu, [b_bank[par]], [b_st32[sgi]])
                tt("dve", H[:, c - c0, 0:nt], st32[sgi][:, 0:nt], banks[par + 1][:, 0:nt], ALU.mult,
                   [b_st32[sgi], b_bank[par + 1]], [b_H[c - c0]])
            for o in range(16):
                i = wload(wd[f][o, :, c0 * 128:c1 * 128], (c1 - c0) * 128)
                wt = wsl[i][:, 0:(c1 - c0) * 128].rearrange("p (c n) -> p c n", n=128)
                bk = 4 + (o % 3)
                for c in range(c0, c1):
                    mm(banks[bk][:, 0:nt], wt[:, c - c0, :], H[:, c - c0, 0:nt], c == c0, c == c1 - 1,
                       [b_w[i], b_H[c - c0]], [b_bank[bk]])
                if hf == 0:
                    cp("act", y1[:, o, 0:nt], banks[bk][:, 0:nt], [b_bank[bk]], [b_y1[o]])
                else:
                    tt("dve", y1[:, o, 0:nt], y1[:, o, 0:nt], banks[bk][:, 0:nt], ALU.add,
                       [b_y1[o], b_bank[bk]], [b_y1[o]])

    def load_x(src, row0, nt):
        nsub = nt // 128
        for sub in range(nsub):
            s = sub % 2
            xst = y1f(s * 2048, 2048)
            dma("sp", xst, src[row0 + sub * 128:row0 + (sub + 1) * 128, :], [], [b_xst[s]], b_xst[s])
            for g in range(4):
                bk = g % 4
                for j in range(4):
                    k = 4 * g + j
                    P.op("pe", lambda e, o=banks[bk][:, j * 128:(j + 1) * 128], i=xst[:, k * 128:(k + 1) * 128]:
                         e.transpose(o, i, ident_f[:]), [b_xst[s], b_const], [b_bank[bk]])
                eng = "act" if g % 2 == 0 else "dve"
                cp(eng, xT[:, 4 * g:4 * g + 4, sub * 128:(sub + 1) * 128],
                   banks[bk][:, :].rearrange("p (j t) -> p j t", j=4), [b_bank[bk]], b_xT[4 * g:4 * g + 4])

    def store_y(dst, row0, nt):
        nsub = nt // 128
        for sub in range(nsub):
            s = sub % 2
            yst = y1f(s * 2048, 2048)
            for g in range(4):
                bk = g % 4
                for j in range(4):
                    k = 4 * g + j
                    P.op("pe", lambda e, o=banks[bk][:, j * 128:(j + 1) * 128], i=xT[:, k, sub * 128:(sub + 1) * 128]:
                         e.transpose(o, i, ident_f[:]), [b_xT[k], b_const], [b_bank[bk]])
                eng = "act" if g % 2 == 0 else "dve"
                cp(eng, yst[:, g * 512:(g + 1) * 512], banks[bk][:, :], [b_bank[bk]], [b_xst[s]])
            dma("pool", dst[row0 + sub * 128:row0 + (sub + 1) * 128, :], yst, [b_xst[s]], [dbuf("yout")], h_yst[s])

    def qkv_proj(kind, nt, pos_row0, tok0, outs, out_row0, kt_dst, v_dst):
        nsub = nt // 128
        cgs = range(12) if kind != "aux" else (2, 3, 4, 5, 8, 9, 10, 11)
        QA_SCALE = 0.125
        QB_SCALE = 128.0 ** -0.5
        for cg in cgs:
            typ = cg // 2
            hh = cg % 2
            for hf in range(2):
                i = wload(wqkv[cg, hf], 4096)
                wt = wsl[i][:, 0:4096].rearrange("p (k n) -> p k n", k=8)
                for sub in range(nsub):
                    for k in range(8):
                        mm(banks[sub][:, :], xn[:, hf * 8 + k, sub * 128:(sub + 1) * 128], wt[:, k, :],
                           hf == 0 and k == 0, hf == 1 and k == 7, [b_w[i], b_xn[hf * 8 + k]], [b_bank[sub]])
            kts = nxt("kt", 2) if typ in (1, 4) else None
            for sub in range(nsub):
                s = nxt("st", 3)
                if typ == 0:
                    P.op("act", lambda e, o=st32[s][:], i=banks[sub][:, :]: e.mul(out=o, in_=i, mul=QA_SCALE),
                         [b_bank[sub]], [b_st32[s]])
                elif typ == 3:
                    P.op("act", lambda e, o=st32[s][:], i=banks[sub][:, :]: e.mul(out=o, in_=i, mul=QB_SCALE),
                         [b_bank[sub]], [b_st32[s]])
                else:
                    cp("act", st32[s][:], banks[sub][:, :], [b_bank[sub]], [b_st32[s]])
                if typ in (0, 1):
                    c = nxt("cs", 2)
                    r0 = pos_row0 + sub * 128
                    dma("pool", cst[c][:], cos_t[r0:r0 + 128, :], [], [b_cs[c]], b_cs[c])
                    dma("pool", snt[c][:], sin_t[r0:r0 + 128, :], [], [b_cs[c]], b_cs[c])
                    v = st32[s][:].rearrange("p (g d) -> p g d", d=64)
                    x1, x2 = v[:, :, 0:8], v[:, :, 8:16]
                    cv = cst[c][:].rearrange("p (g d) -> p g d", d=8)[:, 0:8, :]
                    sv = snt[c][:].rearrange("p (g d) -> p g d", d=8)[:, 0:8, :]
                    r = nxt("rt", 2)
                    t = rtmp[r][:].rearrange("p a (g d) -> p a g d", d=8)
                    rb = [b_st32[s], b_cs[c]]
                    tt("pool", t[:, 0], x1, cv, ALU.mult, rb, [b_rtmp[r]])
                    tt("pool", t[:, 1], x2, sv, ALU.mult, rb, [b_rtmp[r]])
                    tt("pool", t[:, 2], x2, cv, ALU.mult, rb, [b_rtmp[r]])
                    tt("pool", t[:, 3], x1, sv, ALU.mult, rb, [b_rtmp[r]])
                    tt("pool", x1, t[:, 0], t[:, 1], ALU.subtract, [b_rtmp[r]], [b_st32[s]])
                    tt("pool", x2, t[:, 2], t[:, 3], ALU.add, [b_rtmp[r]], [b_st32[s]])
                if typ in (1, 2, 4, 5) and kind != "aux":
                    oi = {1: 0, 2: 1, 4: 2, 5: 3}[typ]
                    r0 = out_row0 + sub * 128
                    dma("pool", outs[oi][r0:r0 + 128, hh * 512:(hh + 1) * 512], st32[s][:], [b_st32[s]],
                        [dbuf("kvout")], b_st32[s])
                cp("dve", stb[s][:], st32[s][:], [b_st32[s]], [b_stb[s]])
                if typ in (2, 5):
                    br = 0 if typ == 2 else 1
                    for (dap, p0, p1) in v_dst(br, hh, sub):
                        dma("pool", dap, stb[s][p0:p1, :].rearrange("p (h e) -> p h e", e=128), [b_stb[s]],
                            [dbuf(("v", kind, br, tok0, hh, sub))], b_stb[s])
                else:
                    bk = 4 + (sub % 2)
                    bv = banks[bk][:, :].bitcast(BF16)
                    for j in range(4):
                        P.op("pe", lambda e, o=bv[:, j * 128:(j + 1) * 128], i=stb[s][:, j * 128:(j + 1) * 128]:
                             e.transpose(o, i, ident_b[:]), [b_stb[s], b_const], [b_bank[bk]])
                    if typ in (0, 3):
                        base = (0 if typ == 0 else 8) + hh * 4
                        cp("dve", H[:, base:base + 4, sub * 128:(sub + 1) * 128],
                           bv[:, 0:512].rearrange("p (j t) -> p j t", j=4), [b_bank[bk]], b_H[base:base + 4])
                    else:
                        cp("dve", ktst[kts][:, :, sub * 128:(sub + 1) * 128],
                           bv[:, 0:512].rearrange("p (j t) -> p j t", j=4), [b_bank[bk]], [b_ktst[kts]])
            if typ in (1, 4):
                br = 0 if typ == 1 else 1
                for (dap, t0, t1) in kt_dst(br, hh):
                    dma("pool", dap, ktst[kts][:, :, t0:t1], [b_ktst[kts]],
                        [dbuf(("kt", kind, br, tok0, hh))], b_ktst[kts])

    rr["kt"] = 0
    rr["cs"] = 0
    rr["rt"] = 0

    def attention(nq, groups, q0=0):
        AM = cfg.get("am", "")
        tiles = []
        for gi, g in enumerate(groups):
            ntl = (g["nk"] + 127) // 128
            for t in range(ntl):
                tiles.append((gi, t, min(128, g["nk"] - t * 128)))
        NTL = len(tiles)

        def load(gi, br, h):
            g = groups[gi]
            sl = nxt("kv", NKV)
            nk = g["nk"]
            ntl = (nk + 127) // 128
            dma("pool", kring[sl][:, 0:nk], g["kt"](br, h), g["bufs"](br), [b_kv[sl]], b_kv[sl])
            dma("pool", vring[sl][:, 0:ntl * 128] if (ntl * 128 == nk or ntl > 1) else vring[sl][0:nk, 0:128],
                g["v"](br, h), g["bufs"](br), [b_kv[sl]], b_kv[sl])
            return sl

        for h in (range(8) if AM == "" else (range(1) if AM.startswith("diff") else range(0))):
            slots = {}

            def qk(i):
                gi, t, n = tiles[i]
                if gi not in slots:
                    slots[gi] = load(gi, 0, h)
                sl = slots[gi]
                par = i % 2
                mk = groups[gi]["mask_d"].get(t)
                for m in range(2):
                    bk = 2 * par + m
                    mm(banks[bk][0:n, 0:nq], kring[sl][64 * m:64 * m + 64, t * 128:t * 128 + n],
                       H[64 * m:64 * m + 64, h, q0:q0 + nq], True, mk is None, [b_kv[sl], b_H[h]], [b_bank[bk]])
                    if mk is not None:
                        mm(banks[bk][0:n, 0:nq], ident_b[0:n, 0:n], maskd[0:n, mk:mk + nq], False, True,
                           [b_const], [b_bank[bk]])

            def ex(i):
                gi, t, n = tiles[i]
                par = i % 2
                for m in range(2):
                    bk = 2 * par + m
                    pw = w_p[m][par]
                    act(pw.ap[0:n, 0:nq], banks[bk][0:n, 0:nq], AF.Exp, [b_bank[bk], b_const], [pw.buf],
                        bias=groups[gi]["bias"][0:n, :])

            def pv(i):
                gi, t, n = tiles[i]
                sl = slots[gi]
                par = i % 2
                for m in range(2):
                    pw = w_p[m][par]
                    mm(banks[4 + m][:, 0:nq], vring[sl][0:n, t * 128:(t + 1) * 128], pw.ap[0:n, 0:nq],
                       i == 0, i == NTL - 1, [b_kv[sl], pw.buf], [b_bank[4 + m]])
                    mm(banks[6 + m][:, 0:nq], ones_b[0:n, :], pw.ap[0:n, 0:nq],
                       i == 0, i == NTL - 1, [b_const, pw.buf], [b_bank[6 + m]])

            qk(0)
            for i in range(NTL):
                ex(i)
                if i + 1 < NTL:
                    qk(i + 1)
                pv(i)
            t1, t2 = w_t[0], w_t[1]
            P.op("dve", lambda e, o=t1.ap[:, 0:nq], i=banks[6][:, 0:nq]: e.reciprocal(out=o, in_=i), [b_bank[6]], [t1.buf])
            tt("dve", t1.ap[:, 0:nq], t1.ap[:, 0:nq], banks[4][:, 0:nq], ALU.mult, [t1.buf, b_bank[4]], [t1.buf])
            P.op("dve", lambda e, o=t2.ap[:, 0:nq], i=banks[7][:, 0:nq]: e.reciprocal(out=o, in_=i), [b_bank[7]], [t2.buf])
            tt("dve", t2.ap[:, 0:nq], t2.ap[:, 0:nq], banks[5][:, 0:nq], ALU.mult, [t2.buf, b_bank[5]], [t2.buf])
            stt("dve", t1.ap[:, 0:nq], t2.ap[:, 0:nq], nlam[:, 0:1], t1.ap[:, 0:nq], ALU.mult, ALU.add,
                [t1.buf, t2.buf, b_const], [t1.buf])
            s = nxt("sq", 2)
            act(sqt[s][:, 0:nq], t1.ap[:, 0:nq], AF.Square, [t1.buf], [b_sq[s]])
            mm(banks[0][:, 0:nq], ones_b[:], sqt[s][:, 0:nq], True, True, [b_sq[s], b_const], [b_bank[0]])
            act(t2.ap[:, 0:nq], banks[0][:, 0:nq], AF.Ln, [b_bank[0], b_const], [t2.buf], bias=epsb[:, 0:1], scale=1.0 / 128)
            act(t2.ap[:, 0:nq], t2.ap[:, 0:nq], AF.Exp, [t2.buf], [t2.buf], scale=-0.5)
            stt("dve", H[:, 16 + h, q0:q0 + nq], t1.ap[:, 0:nq], gsub[:, 0:1], t2.ap[:, 0:nq], ALU.mult, ALU.mult,
                [t1.buf, t2.buf, b_const], [b_H[16 + h]])

        rt = tiles[::-1]
        for hp in (range(4) if AM == "" else (range(1) if AM.startswith("sb") else range(0))):
            hh2 = (2 * hp, 2 * hp + 1)
            slots = [{}, {}]

            def sqk(s_, i):
                gi, t, n = rt[i]
                h = hh2[s_]
                if gi not in slots[s_]:
                    slots[s_][gi] = load(gi, 1, h)
                sl = slots[s_][gi]
                zb = 2 * s_ + (i % 2)
                mk = groups[gi]["mask_s"].get(t)
                mm(banks[zb][0:n, 0:nq], kring[sl][:, t * 128:t * 128 + n], H[:, 8 + h, q0:q0 + nq], True, False,
                   [b_kv[sl], b_H[8 + h]], [b_bank[zb]])
                if mk is not None:
                    mm(banks[zb][0:n, 0:nq], ident_b[0:n, 0:n], masks[0:n, mk:mk + nq], False, False,
                       [b_const], [b_bank[zb]])

            def ses(s_, i):
                gi, t, n = rt[i]
                zb = 2 * s_ + (i % 2)
                ew, spw = w_e[s_], w_sp[s_][i % 2]
                act(ew.ap[0:n, 0:nq], banks[zb][0:n, 0:nq], AF.Exp, [b_bank[zb], b_const], [ew.buf],
                    bias=groups[gi]["bias"][0:n, :])
                act(spw.ap[0:n, 0:nq], ew.ap[0:n, 0:nq], AF.Ln, [ew.buf], [spw.buf], bias=1.0, scale=1.0)

            def stri(s_, i):
                gi, t, n = rt[i]
                zb = 2 * s_ + (i % 2)
                spw = w_sp[s_][i % 2]
                mm(banks[zb][0:n, 0:nq], ntri_b[0:n, 0:n], spw.ap[0:n, 0:nq], False, True,
                   [spw.buf, b_const], [b_bank[zb]])
                if i < NTL - 1:
                    mm(banks[6 + s_][:, 0:nq], ones_b[0:n, :], spw.ap[0:n, 0:nq], i == 0, i == NTL - 2,
                       [spw.buf, b_const], [b_bank[6 + s_]])

            def sdv(s_, i):
                gi, t, n = rt[i]
                zb = 2 * s_ + (i % 2)
                cprev = z0t[:, s_ * 512:s_ * 512 + nq]
                if i > 0:
                    tw = w_tm[s_][i % 2]
                    tt("dve", tw.ap[0:n, 0:nq], banks[zb][0:n, 0:nq], cprev[0:n, :], ALU.subtract,
                       [b_bank[zb], b_z0[s_]], [tw.buf])
                if i < NTL - 1:
                    cp("dve", cprev, banks[6 + s_][:, 0:nq], [b_bank[6 + s_]], [b_z0[s_]])

            def sax(s_, i):
                gi, t, n = rt[i]
                zb = 2 * s_ + (i % 2)
                aw = w_a[s_][i % 2]
                if i == 0:
                    act(aw.ap[0:n, 0:nq], banks[zb][0:n, 0:nq], AF.Exp, [b_bank[zb], b_const], [aw.buf],
                        bias=groups[gi]["bias"][0:n, :])
                else:
                    tw = w_tm[s_][i % 2]
                    act(aw.ap[0:n, 0:nq], tw.ap[0:n, 0:nq], AF.Exp, [tw.buf, b_const], [aw.buf],
                        bias=groups[gi]["bias"][0:n, :])

            def spv(s_, i):
                gi, t, n = rt[i]
                sl = slots[s_][gi]
                aw = w_a[s_][i % 2]
                mm(banks[4 + s_][:, 0:nq], vring[sl][0:n, t * 128:(t + 1) * 128], aw.ap[0:n, 0:nq],
                   i == 0, i == NTL - 1, [b_kv[sl], aw.buf], [b_bank[4 + s_]])

            for s_ in range(2):
                sqk(s_, 0)
            for i in range(NTL):
                for s_ in range(2):
                    ses(s_, i)
                for s_ in range(2):
                    stri(s_, i)
                if i + 1 < NTL:
                    for s_ in range(2):
                        sqk(s_, i + 1)
                for s_ in range(2):
                    sdv(s_, i)
                for s_ in range(2):
                    sax(s_, i)
                for s_ in range(2):
                    spv(s_, i)
            for s_ in range(2):
                h = hh2[s_]
                cp("act", H[:, 24 + h, q0:q0 + nq], banks[4 + s_][:, 0:nq], [b_bank[4 + s_]], [b_H[24 + h]])

    def merge_wo(nt):
        for o in range(16):
            i = wload(wmg[o], 6144)
            wt = wsl[i][:, 0:6144].rearrange("p (k n) -> p k n", n=128)
            par = 4 * (o % 2)
            for k in range(8):
                mm(banks[par][:, 0:nt], wt[:, 32 + k, :], H[:, 16 + k, 0:nt], k == 0, k == 7,
                   [b_w[i], b_H[16 + k]], [b_bank[par]])
            for k in range(KC):
                mm(banks[par + 1][:, 0:nt], wt[:, k, :], xn[:, k, 0:nt], k == 0, k == KC - 1,
                   [b_w[i], b_xn[k]], [b_bank[par + 1]])
            for k in range(8):
                mm(banks[par + 2][:, 0:nt], wt[:, 40 + k, :], H[:, 24 + k, 0:nt], k == 0, k == 7,
                   [b_w[i], b_H[24 + k]], [b_bank[par + 2]])
            for k in range(KC):
                mm(banks[par + 3][:, 0:nt], wt[:, 16 + k, :], xn[:, k, 0:nt], k == 0, k == KC - 1,
                   [b_w[i], b_xn[k]], [b_bank[par + 3]])
            sa, sbw = w_t[0], w_t[1]
            act(sa.ap[:, 0:nt], banks[par + 1][:, 0:nt], AF.Sigmoid, [b_bank[par + 1]], [sa.buf])
            act(sbw.ap[:, 0:nt], banks[par + 3][:, 0:nt], AF.Sigmoid, [b_bank[par + 3]], [sbw.buf])
            tt("dve", sa.ap[:, 0:nt], sa.ap[:, 0:nt], banks[par][:, 0:nt], ALU.mult, [sa.buf, b_bank[par]], [sa.buf])
            tt("dve", sbw.ap[:, 0:nt], sbw.ap[:, 0:nt], banks[par + 2][:, 0:nt], ALU.mult, [sbw.buf, b_bank[par + 2]],
               [sbw.buf])
            tt("dve", H[:, o, 0:nt], sa.ap[:, 0:nt], sbw.ap[:, 0:nt], ALU.add, [sa.buf, sbw.buf], [b_H[o]])
        for o in range(16):
            i = wload(wot[o], 2048)
            wt = wsl[i][:, 0:2048].rearrange("p (k n) -> p k n", n=128)
            bk = o % 4
            for k in range(KC):
                mm(banks[bk][:, 0:nt], wt[:, k, :], H[:, k, 0:nt], k == 0, k == KC - 1, [b_w[i], b_H[k]], [b_bank[bk]])
            cp("act", y1[:, o, 0:nt], banks[bk][:, 0:nt], [b_bank[bk]], [b_y1[o]])

    def convert_cache():
        for sq_ in range(4):
            for br, (ck, cv) in enumerate(((ck_a, cv_a), (ck_b, cv_b))):
                for t in range(8):
                    for isv, src in ((0, ck), (1, cv)):
                        s = nxt("st", 3)
                        for hh in range(2):
                            s = nxt("st", 3)
                            dma("pool", st32[s][:], src[sq_, t * 128:(t + 1) * 128, hh * 512:(hh + 1) * 512], [],
                                [b_st32[s]], b_st32[s])
                            cp("dve", stb[s][:], st32[s][:], [b_st32[s]], [b_stb[s]])
                            if isv:
                                dma("pool", v_s[br][sq_, 4 * hh:4 * hh + 4, :, t * 128:(t + 1) * 128]
                                    .rearrange("h p e -> p h e"),
                                    stb[s][:].rearrange("p (h e) -> p h e", e=128), [b_stb[s]],
                                    [dbuf(("cv", sq_, br))], b_stb[s])
                            else:
                                bk = 4 + (hh % 2)
                                bv = banks[bk][:, :].bitcast(BF16)
                                for j in range(4):
                                    P.op("pe", lambda e, o=bv[:, j * 128:(j + 1) * 128],
                                         i=stb[s][:, j * 128:(j + 1) * 128]: e.transpose(o, i, ident_b[:]),
                                         [b_stb[s], b_const], [b_bank[bk]])
                                kts = nxt("kt", 2)
                                cp("dve", ktst[kts][:, :, 0:128], bv[:, 0:512].rearrange("p (j t) -> p j t", j=4),
                                   [b_bank[bk]], [b_ktst[kts]])
                                dma("pool", kt_s[br][sq_, 4 * hh:4 * hh + 4, :, t * 128:(t + 1) * 128]
                                    .rearrange("h p t -> p h t"), ktst[kts][:, :, 0:128], [b_ktst[kts]],
                                    [dbuf(("ck", sq_, br))], b_ktst[kts])

    def prompt_kt_dst(tok0):
        return lambda br, hh: [(kt_p[br][4 * hh:4 * hh + 4, :, tok0:tok0 + NT].rearrange("h p t -> p h t"), 0, NT)]

    def prompt_v_dst(tok0):
        def f(br, hh, sub):
            tile = tok0 // 128 + sub
            return [(v_p[br][4 * hh:4 * hh + 4, :, tile * 128:(tile + 1) * 128].rearrange("h p e -> p h e"), 0, 128)]
        return f

    def front(kind, src, row0, nt, pos_row0, tok0, outs, out_row0, kt_dst, v_dst):
        load_x(src, row0, nt)
        norm_to_xn(0, nt)
        ffn(1, nt)
        post_residual(1, nt, 0.5)
        norm_to_xn(2, nt)
        qkv_proj(kind, nt, pos_row0, tok0, outs, out_row0, kt_dst, v_dst)

    def back(dst, row0, nt):
        merge_wo(nt)
        post_residual(3, nt, 1.0)
        norm_to_xn(4, nt)
        ffn(2, nt)
        post_residual(5, nt, 0.5)
        store_y(dst, row0, nt)

    STOP = cfg.get("stop", "")
    load_consts()
    if STOP == "consts":
        return nc, P, es
    convert_weights()
    if STOP == "weights":
        return nc, P, es
    if DO_SMP:
        convert_cache()
    for b in range(NB_AUX):
        front("aux", xa, b * NT, NT, b * NT, b * NT, None, 0, prompt_kt_dst(b * NT), prompt_v_dst(b * NT))
    if STOP == "front_aux":
        return nc, P, es
    for b in range(NB_OWN):
        tok0 = 4096 + b * NT
        front("own", xo, b * NT, NT, 4096 + b * NT, tok0, kv_o, b * NT, prompt_kt_dst(tok0), prompt_v_dst(tok0))
        if STOP == "front_own":
            return nc, P, es
        groups = []
        for gi in list(range(NB_AUX)) + [8 + j for j in range(b + 1)]:
            own = gi >= 8
            k0 = gi * 512
            keyb = (lambda br, gi=gi, own=own: [dbuf(("kt", "own" if own else "aux", br, gi * 512, hh)) for hh in range(2)]
                    + [dbuf(("v", "own" if own else "aux", br, gi * 512, hh, sub)) for hh in range(2) for sub in range(4)])
            groups.append(dict(
                nk=512,
                kt=lambda br, h, k0=k0: kt_p[br][h, :, k0:k0 + 512],
                v=lambda br, h, k0=k0: v_p[br][h, :, (k0 // 128) * 128:(k0 // 128 + 4) * 128],
                bufs=keyb,
                bias=zerob if own else auxb,
                mask_d=({t: t * 512 for t in range(4)} if gi == 8 + b else {}),
                mask_s=({t: t * 512 for t in range(4)} if gi == 8 + b else {}),
            ))
        attention(NT, groups)
        if STOP == "attn":
            return nc, P, es
        back(y_o, b * NT, NT)
    if DO_SMP:
        nt = 256

        def s_kt_dst(br, hh):
            return [(kt_s[br][i, 4 * hh:4 * hh + 4, :, 1024:1088].rearrange("h p t -> p h t"), 64 * i, 64 * i + 64)
                    for i in range(4)]

        def s_v_dst(br, hh, sub):
            return [(v_s[br][2 * sub + j, 4 * hh:4 * hh + 4, 0:64, 1024:1152].rearrange("h p e -> p h e"),
                     64 * j, 64 * j + 64) for j in range(2)]

        front("smp", xs, 0, nt, 8192, 0, kv_s, 0, s_kt_dst, s_v_dst)
        for i in range(4):
            groups = []
            for gi in range(3):
                nk = 512 if gi < 2 else 64
                k0 = gi * 512
                if gi < 2:
                    keyb = lambda br, i=i: [dbuf(("ck", i, br)), dbuf(("cv", i, br))]
                else:
                    keyb = lambda br: ([dbuf(("kt", "smp", br, 0, hh)) for hh in range(2)]
                                       + [dbuf(("v", "smp", br, 0, hh, sub)) for hh in range(2) for sub in range(2)])
                groups.append(dict(
                    nk=nk,
                    kt=lambda br, h, i=i, k0=k0, nk=nk: kt_s[br][i, h, :, k0:k0 + nk],
                    v=(lambda br, h, i=i, k0=k0: v_s[br][i, h, :, k0:k0 + 512]) if gi < 2 else
                      (lambda br, h, i=i: v_s[br][i, h, 0:64, 1024:1152]),
                    bufs=keyb, bias=zerob, mask_d={}, mask_s=({0: 0} if gi == 2 else {}),
                ))
            attention(64, groups, q0=64 * i)
        back(y_s, 0, nt)

    return nc, P, es


def emit_program(nc, P, es):
    P.finalize()
    with es:
        sems = {e: es.enter_context(nc.semaphore(f"s_{e}")) for e in ("pe", "act", "dve", "pool")}
        for i, b in enumerate(P.dma_bufs):
            b.sem = es.enter_context(nc.semaphore(f"d{i}"))
        block = es.enter_context(nc.Block())

        def run(name, e):
            waited = {}
            for op in P.ops[name]:
                need = {}
                for d in op.deps:
                    if d.dma is not None:
                        key, sem, val = id(d.dma), d.dma.sem, d.val
                    else:
                        if d.eng == name and name == "pe":
                            continue
                        key, sem, val = d.eng, sems[d.eng], d.val
                    if key not in need or need[key][1] < val:
                        need[key] = (sem, val)
                for key, (sem, val) in need.items():
                    if waited.get(key, 0) < val:
                        e.wait_ge(sem, val)
                        waited[key] = val
                ins = op.fn(e)
                if op.dma is not None:
                    ins.then_inc(op.dma.sem, 16)
                elif op.mark:
                    ins.then_inc(sems[name], 1)
            if name == "sp":
                for b in P.dma_bufs:
                    e.wait_ge(b.sem, 16 * b.ndma)

        @block.sync
        def _(e):
            run("sp", e)

        @block.gpsimd
        def _(e):
            run("pool", e)

        @block.tensor
        def _(e):
            run("pe", e)

        @block.scalar
        def _(e):
            run("act", e)

        @block.vector
        def _(e):
            run("dve", e)
    return nc


def _rope_tables(pos):
    half = 8
    inv_freq = np.power(np.float32(500000.0), -(np.arange(0, 16, 2, dtype=np.float32) / np.float32(16.0))).astype(np.float32)
    ang = pos.astype(np.float32)[:, None] * inv_freq[None, :]
    c = np.cos(ang).astype(np.float32)
    s_ = np.sin(ang).astype(np.float32)
    c = np.ascontiguousarray(np.broadcast_to(c[:, None, :], (pos.shape[0], 16, 8))).reshape(pos.shape[0], 128)
    s_ = np.ascontiguousarray(np.broadcast_to(s_[:, None, :], (pos.shape[0], 16, 8))).reshape(pos.shape[0], 128)
    return c, s_


def _consts():
    ident = np.eye(128, dtype=np.float32)
    j = np.arange(128)[:, None]
    sidx = np.arange(128)[None, :]
    ntri = np.where(j >= sidx, -1.0, 0.0).astype(np.float32)
    k = np.arange(128)[:, None]
    q = np.arange(512)[None, :]
    md = np.zeros((128, 4, 512), np.float32)
    ms = np.zeros((128, 4, 512), np.float32)
    for t in range(4):
        kk = 128 * t + k
        md[:, t, :] = np.where((kk // 64) <= (q // 64), 0.0, NEG_D)
        ms[:, t, :] = np.where(kk < q, 0.0, NEG_S)
    return ident, ntri, md.reshape(128, 2048), ms.reshape(128, 2048)


def make_in_maps(inp):
    f32 = np.float32
    g = lambda k: np.asarray(inp[k], dtype=f32)
    xp, xsm = g("x_prompt"), g("x_sample")
    ident, ntri, md, ms = _consts()
    gains = np.stack([g(n)[0].reshape(16, 128).T for n in (
        "norm_ffn1_pre", "norm_ffn1_post", "norm_mix_pre", "norm_mix_post", "norm_ffn2_pre", "norm_ffn2_post")], axis=1)
    gains = np.ascontiguousarray(gains.reshape(128, 96))
    subg = np.ascontiguousarray(g("subln_g")[0].reshape(128, 1))
    lamv = np.concatenate([g("lam_q1")[0], g("lam_k1")[0], g("lam_q2")[0], g("lam_k2")[0]])[None, :]
    lamv = np.ascontiguousarray(np.broadcast_to(lamv, (128, 256)))
    shared = dict(
        w_in=g("w_in")[0], w_up_a=g("w_up_a")[0], w_up_b=g("w_up_b")[0], w_o=g("w_o")[0],
        f1g=g("ffn1_w_gate")[0], f1u=g("ffn1_w_up")[0], f1d=g("ffn1_w_down")[0],
        f2g=g("ffn2_w_gate")[0], f2u=g("ffn2_w_up")[0], f2d=g("ffn2_w_down")[0],
        gains=gains, subg=subg, lamv=lamv, ident=ident, ntri=ntri, mask_d=md, mask_s=ms,
    )
    cka, cva, ckb, cvb = g("cache_diff_k")[0], g("cache_diff_v")[0], g("cache_sb_k")[0], g("cache_sb_v")[0]
    pos_s = (1024 + np.arange(64)).astype(np.int64)
    maps = []
    for c in range(8):
        seq, half = c // 2, c % 2
        pos = np.concatenate([np.arange(4096), half * 4096 + np.arange(4096), np.tile(pos_s, 4)])
        ct, st_ = _rope_tables(pos)
        m = dict(shared)
        m.update(
            xa=np.ascontiguousarray(xp[seq, 0:4096]),
            xo=np.ascontiguousarray(xp[seq, half * 4096:(half + 1) * 4096]),
            xs=np.ascontiguousarray(xsm[4 * c:4 * c + 4].reshape(256, D)),
            cos_t=ct, sin_t=st_,
            ck_a=np.ascontiguousarray(cka[4 * c:4 * c + 4].reshape(4, 1024, 1024)),
            cv_a=np.ascontiguousarray(cva[4 * c:4 * c + 4].reshape(4, 1024, 1024)),
            ck_b=np.ascontiguousarray(ckb[4 * c:4 * c + 4].reshape(4, 1024, 1024)),
            cv_b=np.ascontiguousarray(cvb[4 * c:4 * c + 4].reshape(4, 1024, 1024)),
            auxb=np.full((128, 1), 0.0 if half == 1 else NEG_D, f32),
        )
        maps.append(m)
    return maps


_PROG_CACHE = {}


def run_cores(inp, cfg=None):
    cfg = dict(CFG) if cfg is None else cfg
    nc, P, es = build_program(cfg)
    emit_program(nc, P, es)
    maps = make_in_maps(inp)
    n = cfg.get("ncores", 8)
    res = run_bass_kernel_spmd(nc, maps[:n], core_ids=list(range(n)))
    return res.results


def kernel(**inp):
    r = run_cores(inp)
    f32 = np.float32
    y_p = np.empty((4, 8192, D), f32)
    y_s = np.empty((32, 64, D), f32)
    pk = [np.empty((1, 4, 8192, 1024), f32) for _ in range(4)]
    sk = [np.empty((1, 32, 64, 1024), f32) for _ in range(4)]
    for c in range(8):
        seq, half = c // 2, c % 2
        sl = slice(half * 4096, (half + 1) * 4096)
        y_p[seq, sl] = r[c]["y_o"]
        y_s[4 * c:4 * c + 4] = np.asarray(r[c]["y_s"]).reshape(4, 64, D)
        for i, n in enumerate(("ka", "va", "kb", "vb")):
            pk[i][0, seq, sl] = r[c][n + "_o"]
            sk[i][0, 4 * c:4 * c + 4] = np.asarray(r[c][n + "_s"]).reshape(4, 64, 1024)
    return (y_p, y_s,
            pk[0].reshape(1, 4, 8192, 8, 2, 64), pk[1].reshape(1, 4, 8192, 8, 128),
            pk[2].reshape(1, 4, 8192, 8, 128), pk[3].reshape(1, 4, 8192, 8, 128),
            sk[0].reshape(1, 32, 64, 8, 2, 64), sk[1].reshape(1, 32, 64, 8, 128),
            sk[2].reshape(1, 32, 64, 8, 128), sk[3].reshape(1, 32, 64, 8, 128))
```

```python
import math
from contextlib import ExitStack

import numpy as np
import concourse.bass as bass
import concourse.mybir as mybir
from concourse.bass_utils import run_bass_kernel_spmd

F32 = mybir.dt.float32
BF16 = mybir.dt.bfloat16
AF = mybir.ActivationFunctionType
ALU = mybir.AluOpType
AX = mybir.AxisListType

D = 2048
KC = 16
DFF = 5504
NCF = 43
HALF0 = 22
NT = 512
EPS = 1e-6
LAM_INIT = 0.8 - 0.6 * math.exp(0.0)
NEG_D = -30000.0
NEG_S = -1024.0

CFG = dict(nb_own=8, nb_aux=8, do_smp=True)


class Buf:
    __slots__ = ("name", "writers", "readers", "overlaps", "sem", "ndma", "queue")

    def __init__(self, name):
        self.name = name
        self.writers = []
        self.readers = []
        self.overlaps = []
        self.sem = None
        self.ndma = 0
        self.queue = None


def overlap(a, b):
    a.overlaps.append(b)
    b.overlaps.append(a)


class Op:
    __slots__ = ("eng", "fn", "deps", "dma", "mark", "val")

    def __init__(self, eng, fn, deps, dma):
        self.eng = eng
        self.fn = fn
        self.deps = deps
        self.dma = dma
        self.mark = False
        self.val = 0


class Prog:
    def __init__(self):
        self.ops = {e: [] for e in ("pe", "act", "dve", "pool", "sp")}
        self.dma_bufs = []

    def op(self, eng, fn, reads=(), writes=(), dma=None):
        deps = set()
        for b in reads:
            deps.update(b.writers)
            for o in b.overlaps:
                deps.update(o.writers)
        for b in writes:
            deps.update(b.writers)
            deps.update(b.readers)
            for o in b.overlaps:
                deps.update(o.writers)
                deps.update(o.readers)
        op = Op(eng, fn, deps, dma)
        for b in reads:
            b.readers.append(op)
        for b in writes:
            b.writers = [op]
            b.readers = []
        if dma is not None:
            if dma.queue is None:
                dma.queue = eng
                self.dma_bufs.append(dma)
            assert dma.queue == eng, (dma.name, dma.queue, eng)
            dma.ndma += 1
            op.val = 16 * dma.ndma
        self.ops[eng].append(op)
        return op

    def finalize(self):
        for e, lst in self.ops.items():
            for op in lst:
                for d in op.deps:
                    d.mark = True
        for e, lst in self.ops.items():
            n = 0
            for op in lst:
                if op.dma is None and op.mark:
                    n += 1
                    op.val = n


def build_program(cfg):
    NB_OWN, NB_AUX, DO_SMP = cfg["nb_own"], cfg["nb_aux"], cfg["do_smp"]
    nc = bass.Bass("TRN2", target_bir_lowering=False)
    P = Prog()
    es = ExitStack()

    def din(name, shape, dt=F32):
        return nc.dram_tensor(name, list(shape), dt, kind="ExternalInput").ap()

    def dout(name, shape, dt=F32):
        return nc.dram_tensor(name, list(shape), dt, kind="ExternalOutput").ap()

    def dscr(name, shape, dt=BF16):
        return nc.dram_tensor(name, list(shape), dt).ap()

    xa = din("xa", [4096, D])
    xo = din("xo", [4096, D])
    xs = din("xs", [256, D])
    cos_t = din("cos_t", [8448, 128])
    sin_t = din("sin_t", [8448, 128])
    ck_a = din("ck_a", [4, 1024, 1024])
    cv_a = din("cv_a", [4, 1024, 1024])
    ck_b = din("ck_b", [4, 1024, 1024])
    cv_b = din("cv_b", [4, 1024, 1024])
    w_in = din("w_in", [D, 10240])
    w_up_a = din("w_up_a", [1024, D])
    w_up_b = din("w_up_b", [1024, D])
    w_o = din("w_o", [D, D])
    fw = {}
    for f in (1, 2):
        fw[f] = (din(f"f{f}g", [D, DFF]), din(f"f{f}u", [D, DFF]), din(f"f{f}d", [DFF, D]))
    gains_d = din("gains", [128, 96])
    subg_d = din("subg", [128, 1])
    lam_d = din("lamv", [128, 256])
    ident_d = din("ident", [128, 128])
    ntri_d = din("ntri", [128, 128])
    maskd_d = din("mask_d", [128, 2048])
    masks_d = din("mask_s", [128, 2048])
    auxb_d = din("auxb", [128, 1])
    y_o = dout("y_o", [4096, D])
    y_s = dout("y_s", [256, D])
    kv_o = [dout(n, [4096, 1024]) for n in ("ka_o", "va_o", "kb_o", "vb_o")]
    kv_s = [dout(n, [256, 1024]) for n in ("ka_s", "va_s", "kb_s", "vb_s")]
    wgu = {f: dscr(f"wgu{f}", [NCF, 128, 2 * KC * 128]) for f in (1, 2)}
    wd = {f: dscr(f"wd{f}", [16, 128, NCF * 128]) for f in (1, 2)}
    wqkv = dscr("wqkv", [12, 2, 128, 8 * 512])
    wmg = dscr("wmg", [16, 128, 48 * 128])
    wot = dscr("wot", [16, 128, KC * 128])
    kt_p = [dscr(f"kt_p{i}", [8, 128, 8192]) for i in range(2)]
    v_p = [dscr(f"v_p{i}", [8, 128, 64 * 128]) for i in range(2)]
    kt_s = [dscr(f"kt_s{i}", [4, 8, 128, 1152]) for i in range(2)]
    v_s = [dscr(f"v_s{i}", [4, 8, 128, 9 * 128]) for i in range(2)]

    def sb(name, shape, dt):
        return es.enter_context(nc.sbuf_tensor(name, list(shape), dt))

    xT = sb("xT", [128, KC, NT], F32)
    xn = sb("xn", [128, KC, NT], BF16)
    H = sb("H", [128, 32, NT], BF16)
    y1 = sb("y1", [128, KC, NT], F32)
    NW = 3
    wsl = [sb(f"wsl{i}", [128, 6144], BF16) for i in range(NW)]
    sqt = [sb(f"sq{i}", [128, NT], BF16) for i in range(2)]
    rstd = sb("rstd", [128, NT], F32)
    lnt = sb("lnt", [128, NT], F32)
    st32 = [sb(f"st32_{i}", [128, 512], F32) for i in range(3)]
    stb = [sb(f"stb_{i}", [128, 512], BF16) for i in range(3)]
    ktst = [sb(f"ktst{i}", [128, 4, NT], BF16) for i in range(2)]
    cst = [sb(f"cos{i}", [128, 128], F32) for i in range(2)]
    snt = [sb(f"sin{i}", [128, 128], F32) for i in range(2)]
    rtmp = [sb(f"rtmp{i}", [128, 4, 64], F32) for i in range(2)]
    NKV = 6
    kring = [sb(f"kring{i}", [128, 512], BF16) for i in range(NKV)]
    vring = [sb(f"vring{i}", [128, 512], BF16) for i in range(NKV)]
    gains = sb("gains_sb", [128, 96], F32)
    subg = sb("subg_sb", [128, 1], F32)
    gsub = sb("gsub", [128, 1], F32)
    lamv = sb("lam_sb", [128, 256], F32)
    lamp = sb("lamp", [128, 64], F32)
    lams = sb("lams", [128, 4], F32)
    nlam = sb("nlam", [128, 1], F32)
    ident_f = sb("ident_f", [128, 128], F32)
    ident_b = sb("ident_b", [128, 128], BF16)
    ntri_f = sb("ntri_f", [128, 128], F32)
    ntri_b = sb("ntri_b", [128, 128], BF16)
    ones_b = sb("ones_b", [128, 128], BF16)
    nones_b = sb("nones_b", [1, 128], BF16)
    maskd = sb("maskd", [128, 2048], BF16)
    masks = sb("masks", [128, 2048], BF16)
    auxb = sb("auxb_sb", [128, 1], F32)
    zerob = sb("zerob", [128, 1], F32)
    epsb = sb("epsb", [128, 1], F32)
    y1flat = y1[:].rearrange("p a b -> p (a b)")
    y1b = y1flat.bitcast(BF16)

    def y1f(off, n):
        return y1flat[:, off:off + n]

    def y1h(off, n):
        return y1b[:, 2 * off:2 * off + n]

    pairs = [es.enter_context(nc.psum_tensor(f"pair{i}", [128, 1024], F32)) for i in range(4)]
    banks = [pairs[i // 2][:, (i % 2) * 512:(i % 2 + 1) * 512] for i in range(8)]

    B = lambda n: Buf(n)
    b_xT = [B(f"xT{k}") for k in range(KC)]
    b_xn = [B(f"xn{k}") for k in range(KC)]
    b_H = [B(f"H{k}") for k in range(32)]
    b_y1 = [B(f"y1_{k}") for k in range(KC)]
    b_w = [B(f"w{i}") for i in range(NW)]
    b_sq = [B(f"sq{i}") for i in range(2)]
    b_rstd, b_lnt = B("rstd"), B("lnt")
    b_st32 = [B(f"st32_{i}") for i in range(3)]
    b_stb = [B(f"stb{i}") for i in range(3)]
    b_ktst = [B(f"ktst{i}") for i in range(2)]
    b_cs = [B(f"cs{i}") for i in range(2)]
    b_rtmp = [B(f"rtmp{i}") for i in range(2)]
    b_kv = [B(f"kv{i}") for i in range(NKV)]
    b_const = B("const")
    b_bank = [B(f"bank{i}") for i in range(8)]
    b_xst = [B("xst0"), B("xst1")]
    h_yst = [B("hyst0"), B("hyst1")]
    for s in range(2):
        for k in range(4):
            overlap(b_xst[s], b_y1[4 * s + k])
    WOFF = 0

    class WT:
        def __init__(self, name, off, n, dt):
            self.buf = B(name)
            self.ap = y1f(off, n) if dt == F32 else y1h(off, n)
            k0 = off // 512
            k1 = (off + (n if dt == F32 else (n + 1) // 2) - 1) // 512
            for k in range(k0, k1 + 1):
                overlap(self.buf, b_y1[k])
            for s_ in range(2):
                if k0 < 4 * s_ + 4 and k1 >= 4 * s_:
                    overlap(self.buf, b_xst[s_])

    woff = [WOFF]

    def walloc(name, n, dt):
        w = WT(name, woff[0], n, dt)
        woff[0] += n if dt == F32 else n // 2
        assert woff[0] <= 16 * 512
        return w

    w_pp = [walloc(f"pp{s}", 1024, BF16) for s in range(2)]

    class _V:
        def __init__(self, ap, buf):
            self.ap, self.buf = ap, buf
    w_p = [[_V(w_pp[s].ap[:, m * 512:(m + 1) * 512], w_pp[s].buf) for s in range(2)] for m in range(2)]
    w_e2 = walloc("e2", 1024, F32)
    w_sp2 = [walloc(f"sp2{q}", 1024, BF16) for q in range(2)]
    w_a2 = [walloc(f"a2{q}", 1024, BF16) for q in range(2)]
    w_tm2 = [walloc(f"tm2{q}", 1024, F32) for q in range(2)]
    w_cp = walloc("cprev2", 1024, F32)
    w_t = [_V(w_tm2[0].ap[:, 0:512], w_tm2[0].buf), _V(w_tm2[1].ap[:, 0:512], w_tm2[1].buf),
           _V(w_e2.ap[:, 0:512], w_e2.buf)]

    dbufs = {}

    def dbuf(key):
        if key not in dbufs:
            dbufs[key] = B(str(key))
        return dbufs[key]

    rr = {"w": 0, "sq": 0, "st": 0, "kv": 0, "cast": 0}

    def nxt(key, n):
        v = rr[key]
        rr[key] = (v + 1) % n
        return v

    def dma(eng, out, in_, reads, writes, owner):
        return P.op(eng, lambda e, o=out, i=in_: e.dma_start(out=o, in_=i), reads, writes, dma=owner)

    def mm(out, lhsT, rhs, start, stop, reads, writes):
        return P.op("pe", lambda e, o=out, l=lhsT, r=rhs, s=start, t=stop: e.matmul(o, l, r, start=s, stop=t), reads, writes)

    def act(out, in_, func, reads, writes, bias=None, scale=None):
        kw = {}
        if bias is not None:
            kw["bias"] = bias
        if scale is not None:
            kw["scale"] = scale
        return P.op("act", lambda e, o=out, i=in_, f=func, k=kw: e.activation(out=o, in_=i, func=f, **k), reads, writes)

    def cp(eng, out, in_, reads, writes):
        if eng == "act":
            return P.op("act", lambda e, o=out, i=in_: e.copy(out=o, in_=i), reads, writes)
        return P.op(eng, lambda e, o=out, i=in_: e.tensor_copy(out=o, in_=i), reads, writes)

    def tt(eng, out, in0, in1, op, reads, writes):
        return P.op(eng, lambda e, o=out, a=in0, b=in1, p=op: e.tensor_tensor(out=o, in0=a, in1=b, op=p), reads, writes)

    def stt(eng, out, in0, scalar, in1, op0, op1, reads, writes):
        return P.op(eng, lambda e, o=out, a=in0, s=scalar, b=in1, p0=op0, p1=op1:
                    e.scalar_tensor_tensor(out=o, in0=a, scalar=s, in1=b, op0=p0, op1=p1), reads, writes)

    def load_consts():
        for dst, src in ((gains, gains_d), (subg, subg_d), (lamv, lam_d), (ident_f, ident_d), (ntri_f, ntri_d),
                         (auxb, auxb_d)):
            dma("sp", dst[:], src[:, :], [], [b_const], b_const)
        P.op("dve", lambda e: e.memset(ones_b[:], 1.0), [], [b_const])
        P.op("dve", lambda e: e.memset(nones_b[:], -1.0), [], [b_const])
        P.op("dve", lambda e: e.memset(zerob[:], 0.0), [], [b_const])
        cp("dve", ident_b[:], ident_f[:], [b_const], [b_const])
        cp("dve", ntri_b[:], ntri_f[:], [b_const], [b_const])
        mkf = H[:].rearrange("p a b -> p (a b)").bitcast(F32)
        dma("sp", mkf[:, 0:2048], maskd_d[:, :], [], [b_const] + b_H[0:8], b_const)
        cp("dve", maskd[:], mkf[:, 0:2048], [b_const] + b_H[0:8], [b_const])
        dma("sp", mkf[:, 2048:4096], masks_d[:, :], [], [b_const] + b_H[8:16], b_const)
        cp("dve", masks[:], mkf[:, 2048:4096], [b_const] + b_H[8:16], [b_const])
        P.op("dve", lambda e: e.memset(epsb[:], EPS), [], [b_const])
        P.op("dve", lambda e: e.tensor_scalar(out=gsub[:], in0=subg[:], scalar1=1.0 - LAM_INIT, scalar2=None,
                                              op0=ALU.mult), [b_const], [b_const])
        for i in range(2):
            tt("dve", lamp[:], lamv[:, 128 * i:128 * i + 64], lamv[:, 128 * i + 64:128 * i + 128], ALU.mult,
               [b_const], [b_const])
            P.op("dve", lambda e, i=i: e.tensor_reduce(out=lams[:, i:i + 1], in_=lamp[:], axis=AX.X, op=ALU.add),
                 [b_const], [b_const])
            act(lams[:, 2 + i:3 + i], lams[:, i:i + 1], AF.Exp, [b_const], [b_const])
        tt("dve", nlam[:], lams[:, 3:4], lams[:, 2:3], ALU.subtract, [b_const], [b_const])
        P.op("dve", lambda e: e.tensor_scalar(out=nlam[:], in0=nlam[:], scalar1=-LAM_INIT, scalar2=None, op0=ALU.add),
             [b_const], [b_const])

    def convert_weights():
        Hf = H[:].rearrange("p a b -> p (a b)").bitcast(F32)
        xnf = xn[:].rearrange("p a b -> p (a b)")
        xTb = xT[:].rearrange("p a b -> p (a b)").bitcast(BF16)
        sin_ = [Hf[:, 0:8192], y1flat[:, 0:8192]]
        sout = [xnf[:, 0:8192], xTb[:, 0:8192]]
        bin_ = [b_H[0:32], b_y1[0:16]]
        bout = [b_xn[0:16], b_xT[0:8]]
        own_in = [B("cvin0"), B("cvin1")]
        own_out = [B("cvout0"), B("cvout1")]
        cnt = [0]

        def slab(src_ap_3d, dims_in, cast_in_view, cast_out_view, dst_ap, n_el, store_view=None):
            i = cnt[0] % 2
            cnt[0] += 1
            a, b_ = dims_in
            dma("sp", sin_[i][:, 0:n_el].rearrange("p (a b) -> p a b", a=a), src_ap_3d, [], bin_[i], own_in[i])
            ceng = ("dve", "pool", "act")[nxt("cast", 3)]
            cp(ceng, cast_out_view(sout[i]), cast_in_view(sin_[i]), bin_[i], bout[i])
            sv = sout[i][:, 0:n_el] if store_view is None else store_view(sout[i][:, 0:n_el])
            dma("act", dst_ap, sv, bout[i], [dbuf("w")], own_out[i])

        def stationary(src, K, col0, ncols, dst, slot, nslot, c_base=0):
            kc = K // 128
            sw = 8192 // kc
            nch = sw // 128
            for s0 in range(0, ncols, sw):
                w = min(sw, ncols - s0)
                nc_ = w // 128
                src3 = src[:, col0 + s0:col0 + s0 + w].rearrange("(k p) n -> p k n", p=128)
                c0 = c_base + s0 // 128
                d = dst[c0:c0 + nc_, :, slot * kc * 128:(slot + 1) * kc * 128].rearrange("c p f -> p c f")
                slab(src3, (kc, w),
                     lambda t, kc=kc, w=w, nc_=nc_: t[:, 0:kc * w].rearrange("p (k c n) -> p c k n", k=kc, c=nc_),
                     lambda t, kc=kc, w=w, nc_=nc_: t[:, 0:kc * w].rearrange("p (c k n) -> p c k n", c=nc_, k=kc),
                     d, kc * w, lambda t, nc_=nc_: t.rearrange("p (c f) -> p c f", c=nc_))

        for f in (1, 2):
            g, u, dn = fw[f]
            stationary(g, D, 0, DFF, wgu[f], 0, 2)
            stationary(u, D, 0, DFF, wgu[f], 1, 2)
            for cc0 in range(0, NCF, 4):
                ncc = min(4, NCF - cc0)
                src3 = dn[cc0 * 128:(cc0 + ncc) * 128, :].rearrange("(c p) n -> p c n", p=128)
                d = wd[f][:, :, cc0 * 128:(cc0 + ncc) * 128].rearrange("o p f -> p o f")
                slab(src3, (ncc, D),
                     lambda t, ncc=ncc: t[:, 0:ncc * D].rearrange("p (c o n) -> p o c n", c=ncc, o=16),
                     lambda t, ncc=ncc: t[:, 0:ncc * D].rearrange("p (o c n) -> p o c n", o=16, c=ncc),
                     d, ncc * D, lambda t: t.rearrange("p (o f) -> p o f", o=16))
        stationary(w_in, D, 6144, 2048, wmg[:, :, 0:2048], 0, 1)
        stationary(w_in, D, 8192, 2048, wmg[:, :, 2048:4096], 0, 1)
        stationary(w_up_a, 1024, 0, 2048, wmg[:, :, 4096:5120], 0, 1)
        stationary(w_up_b, 1024, 0, 2048, wmg[:, :, 5120:6144], 0, 1)
        stationary(w_o, D, 0, 2048, wot, 0, 1)
        for cg in range(12):
            for hf in range(2):
                src3 = w_in[hf * 1024:(hf + 1) * 1024, cg * 512:(cg + 1) * 512].rearrange("(k p) n -> p k n", p=128)
                slab(src3, (8, 512), lambda t: t[:, 0:4096], lambda t: t[:, 0:4096], wqkv[cg, hf], 4096)

    def wload(src_ap, n):
        i = nxt("w", NW)
        dma("sp", wsl[i][:, 0:n], src_ap, [dbuf("w")], [b_w[i]], b_w[i])
        return i

    def rms_stats(src_ap_fn, src_bufs, nt, nfeat_chunks, dim, bank):
        for k in range(nfeat_chunks):
            s = nxt("sq", 2)
            act(sqt[s][:, 0:nt], src_ap_fn(k), AF.Square, [src_bufs[k]], [b_sq[s]])
            mm(banks[bank][:, 0:nt], ones_b[:], sqt[s][:, 0:nt], k == 0, k == nfeat_chunks - 1,
               [b_sq[s], b_const], [b_bank[bank]])
        act(lnt[:, 0:nt], banks[bank][:, 0:nt], AF.Ln, [b_bank[bank], b_const], [b_lnt], bias=epsb[:, 0:1], scale=1.0 / dim)
        act(rstd[:, 0:nt], lnt[:, 0:nt], AF.Exp, [b_lnt], [b_rstd], scale=-0.5)

    def norm_to_xn(gidx, nt):
        rms_stats(lambda k: xT[:, k, 0:nt], b_xT, nt, KC, D, 7)
        for k in range(KC):
            stt("dve", xn[:, k, 0:nt], xT[:, k, 0:nt], gains[:, gidx * 16 + k:gidx * 16 + k + 1], rstd[:, 0:nt],
                ALU.mult, ALU.mult, [b_xT[k], b_rstd, b_const], [b_xn[k]])

    def post_residual(gidx, nt, half):
        rms_stats(lambda k: y1[:, k, 0:nt], b_y1, nt, KC, D, 7)
        for k in range(KC):
            stt("dve", y1[:, k, 0:nt], y1[:, k, 0:nt], gains[:, gidx * 16 + k:gidx * 16 + k + 1], rstd[:, 0:nt],
                ALU.mult, ALU.mult, [b_y1[k], b_rstd, b_const], [b_y1[k]])
            stt("dve", xT[:, k, 0:nt], y1[:, k, 0:nt], half, xT[:, k, 0:nt], ALU.mult, ALU.add,
                [b_y1[k], b_xT[k]], [b_xT[k]])

    def ffn(f, nt, hook=None):
        for hf, (c0, c1) in enumerate(((0, HALF0), (HALF0, NCF))):
            for c in range(c0, c1):
                i = wload(wgu[f][c], 4096)
                wt = wsl[i][:, 0:4096].rearrange("p (s k n) -> p s k n", s=2, k=KC)
                par = (c % 2) * 2
                for s in range(2):
                    for k in range(KC):
                        mm(banks[par + s][:, 0:nt], wt[:, s, k, :], xn[:, k, 0:nt], k == 0, k == KC - 1,
                           [b_w[i], b_xn[k]], [b_bank[par + s]])
                sgi = c % 2
                act(st32[sgi][:, 0:nt], banks[par][:, 0:nt], AF.Silu, [b_bank[par]], [b_st32[sgi]])
                tt("dve", H[:, c - c0, 0:nt], st32[sgi][:, 0:nt], banks[par + 1][:, 0:nt], ALU.mult,
                   [b_st32[sgi], b_bank[par + 1]], [b_H[c - c0]])
                if hook is not None:
                    hook()
            for o in range(16):
                i = wload(wd[f][o, :, c0 * 128:c1 * 128], (c1 - c0) * 128)
                wt = wsl[i][:, 0:(c1 - c0) * 128].rearrange("p (c n) -> p c n", n=128)
                bk = 4 + (o % 3)
                for c in range(c0, c1):
                    mm(banks[bk][:, 0:nt], wt[:, c - c0, :], H[:, c - c0, 0:nt], c == c0, c == c1 - 1,
                       [b_w[i], b_H[c - c0]], [b_bank[bk]])
                if hf == 0:
                    cp("act", y1[:, o, 0:nt], banks[bk][:, 0:nt], [b_bank[bk]], [b_y1[o]])
                else:
                    tt("dve", y1[:, o, 0:nt], y1[:, o, 0:nt], banks[bk][:, 0:nt], ALU.add,
                       [b_y1[o], b_bank[bk]], [b_y1[o]])

    def load_x(src, row0, nt):
        nsub = nt // 128
        for sub in range(nsub):
            s = sub % 2
            xst = y1f(s * 2048, 2048)
            dma("sp", xst, src[row0 + sub * 128:row0 + (sub + 1) * 128, :], [], [b_xst[s]], b_xst[s])
            for g in range(4):
                bk = g % 4
                for j in range(4):
                    k = 4 * g + j
                    P.op("pe", lambda e, o=banks[bk][:, j * 128:(j + 1) * 128], i=xst[:, k * 128:(k + 1) * 128]:
                         e.transpose(o, i, ident_f[:]), [b_xst[s], b_const], [b_bank[bk]])
                eng = "act" if g % 2 == 0 else "dve"
                cp(eng, xT[:, 4 * g:4 * g + 4, sub * 128:(sub + 1) * 128],
                   banks[bk][:, :].rearrange("p (j t) -> p j t", j=4), [b_bank[bk]], b_xT[4 * g:4 * g + 4])

    def store_y(dst, row0, nt):
        nsub = nt // 128
        for sub in range(nsub):
            s = sub % 2
            yst = y1f(s * 2048, 2048)
            for g in range(4):
                bk = g % 4
                for j in range(4):
                    k = 4 * g + j
                    P.op("pe", lambda e, o=banks[bk][:, j * 128:(j + 1) * 128], i=xT[:, k, sub * 128:(sub + 1) * 128]:
                         e.transpose(o, i, ident_f[:]), [b_xT[k], b_const], [b_bank[bk]])
                eng = "act" if g % 2 == 0 else "dve"
                cp(eng, yst[:, g * 512:(g + 1) * 512], banks[bk][:, :], [b_bank[bk]], [b_xst[s]])
            dma("pool", dst[row0 + sub * 128:row0 + (sub + 1) * 128, :], yst, [b_xst[s]], [dbuf("yout")], h_yst[s])

    def qkv_proj(kind, nt, pos_row0, tok0, outs, out_row0, kt_dst, v_dst):
        nsub = nt // 128
        cgs = range(12) if kind != "aux" else (2, 3, 4, 5, 8, 9, 10, 11)
        QA_SCALE = 0.125
        QB_SCALE = 128.0 ** -0.5
        for cg in cgs:
            typ = cg // 2
            hh = cg % 2
            for hf in range(2):
                i = wload(wqkv[cg, hf], 4096)
                wt = wsl[i][:, 0:4096].rearrange("p (k n) -> p k n", k=8)
                for sub in range(nsub):
                    for k in range(8):
                        mm(banks[sub][:, :], xn[:, hf * 8 + k, sub * 128:(sub + 1) * 128], wt[:, k, :],
                           hf == 0 and k == 0, hf == 1 and k == 7, [b_w[i], b_xn[hf * 8 + k]], [b_bank[sub]])
            kts = nxt("kt", 2) if typ in (1, 4) else None
            for sub in range(nsub):
                s = nxt("st", 3)
                if typ == 0:
                    P.op("act", lambda e, o=st32[s][:], i=banks[sub][:, :]: e.mul(out=o, in_=i, mul=QA_SCALE),
                         [b_bank[sub]], [b_st32[s]])
                elif typ == 3:
                    P.op("act", lambda e, o=st32[s][:], i=banks[sub][:, :]: e.mul(out=o, in_=i, mul=QB_SCALE),
                         [b_bank[sub]], [b_st32[s]])
                else:
                    cp("act", st32[s][:], banks[sub][:, :], [b_bank[sub]], [b_st32[s]])
                if typ in (0, 1):
                    c = nxt("cs", 2)
                    r0 = pos_row0 + sub * 128
                    dma("pool", cst[c][:], cos_t[r0:r0 + 128, :], [], [b_cs[c]], b_cs[c])
                    dma("pool", snt[c][:], sin_t[r0:r0 + 128, :], [], [b_cs[c]], b_cs[c])
                    v = st32[s][:].rearrange("p (g d) -> p g d", d=64)
                    x1, x2 = v[:, :, 0:8], v[:, :, 8:16]
                    cv = cst[c][:].rearrange("p (g d) -> p g d", d=8)[:, 0:8, :]
                    sv = snt[c][:].rearrange("p (g d) -> p g d", d=8)[:, 0:8, :]
                    r = nxt("rt", 2)
                    t = rtmp[r][:].rearrange("p a (g d) -> p a g d", d=8)
                    rb = [b_st32[s], b_cs[c]]
                    tt("pool", t[:, 0], x1, cv, ALU.mult, rb, [b_rtmp[r]])
                    tt("pool", t[:, 1], x2, sv, ALU.mult, rb, [b_rtmp[r]])
                    tt("pool", t[:, 2], x2, cv, ALU.mult, rb, [b_rtmp[r]])
                    tt("pool", t[:, 3], x1, sv, ALU.mult, rb, [b_rtmp[r]])
                    tt("pool", x1, t[:, 0], t[:, 1], ALU.subtract, [b_rtmp[r]], [b_st32[s]])
                    tt("pool", x2, t[:, 2], t[:, 3], ALU.add, [b_rtmp[r]], [b_st32[s]])
                if typ in (1, 2, 4, 5) and kind != "aux":
                    oi = {1: 0, 2: 1, 4: 2, 5: 3}[typ]
                    r0 = out_row0 + sub * 128
                    dma("pool", outs[oi][r0:r0 + 128, hh * 512:(hh + 1) * 512], st32[s][:], [b_st32[s]],
                        [dbuf("kvout")], b_st32[s])
                cp("dve", stb[s][:], st32[s][:], [b_st32[s]], [b_stb[s]])
                if typ in (2, 5):
                    br = 0 if typ == 2 else 1
                    for (dap, p0, p1) in v_dst(br, hh, sub):
                        dma("pool", dap, stb[s][p0:p1, :].rearrange("p (h e) -> p h e", e=128), [b_stb[s]],
                            [dbuf(("v", kind, br, tok0, hh, sub))], b_stb[s])
                else:
                    bk = 4 + (sub % 2)
                    bv = banks[bk][:, :].bitcast(BF16)
                    for j in range(4):
                        P.op("pe", lambda e, o=bv[:, j * 128:(j + 1) * 128], i=stb[s][:, j * 128:(j + 1) * 128]:
                             e.transpose(o, i, ident_b[:]), [b_stb[s], b_const], [b_bank[bk]])
                    if typ in (0, 3):
                        base = (0 if typ == 0 else 8) + hh * 4
                        cp("dve", H[:, base:base + 4, sub * 128:(sub + 1) * 128],
                           bv[:, 0:512].rearrange("p (j t) -> p j t", j=4), [b_bank[bk]], b_H[base:base + 4])
                    else:
                        cp("dve", ktst[kts][:, :, sub * 128:(sub + 1) * 128],
                           bv[:, 0:512].rearrange("p (j t) -> p j t", j=4), [b_bank[bk]], [b_ktst[kts]])
            if typ in (1, 4):
                br = 0 if typ == 1 else 1
                for (dap, t0, t1) in kt_dst(br, hh):
                    dma("pool", dap, ktst[kts][:, :, t0:t1], [b_ktst[kts]],
                        [dbuf(("kt", kind, br, tok0, hh))], b_ktst[kts])

    rr["kt"] = 0
    rr["cs"] = 0
    rr["rt"] = 0

    def attention(nq, groups, q0=0):
        AM = cfg.get("am", "")
        tiles = []
        for gi, g in enumerate(groups):
            ntl = (g["nk"] + 127) // 128
            for t in range(ntl):
                tiles.append((gi, t, min(128, g["nk"] - t * 128)))
        NTL = len(tiles)

        def load(gi, br, h):
            g = groups[gi]
            sl = nxt("kv", NKV)
            nk = g["nk"]
            ntl = (nk + 127) // 128
            dma("pool", kring[sl][:, 0:nk], g["kt"](br, h), g["bufs"](br), [b_kv[sl]], b_kv[sl])
            dma("pool", vring[sl][:, 0:ntl * 128] if (ntl * 128 == nk or ntl > 1) else vring[sl][0:nk, 0:128],
                g["v"](br, h), g["bufs"](br), [b_kv[sl]], b_kv[sl])
            return sl

        for h in (range(8) if AM == "" else (range(1) if AM.startswith("diff") else range(0))):
            slots = {}

            def qk(i):
                gi, t, n = tiles[i]
                if gi not in slots:
                    slots[gi] = load(gi, 0, h)
                sl = slots[gi]
                par = i % 2
                mk = groups[gi]["mask_d"].get(t)
                for m in range(2):
                    bk = 2 * par + m
                    mm(banks[bk][0:n, 0:nq], kring[sl][64 * m:64 * m + 64, t * 128:t * 128 + n],
                       H[64 * m:64 * m + 64, h, q0:q0 + nq], True, mk is None, [b_kv[sl], b_H[h]], [b_bank[bk]])
                    if mk is not None:
                        mm(banks[bk][0:n, 0:nq], ident_b[0:n, 0:n], maskd[0:n, mk:mk + nq], False, True,
                           [b_const], [b_bank[bk]])

            def ex(i):
                gi, t, n = tiles[i]
                par = i % 2
                pin = pairs[par][0:n, :].rearrange("p (m q) -> p m q", m=2)[:, :, 0:nq]
                pout = w_pp[par].ap[0:n, :].rearrange("p (m q) -> p m q", m=2)[:, :, 0:nq]
                act(pout, pin, AF.Exp, [b_bank[2 * par], b_bank[2 * par + 1], b_const], [w_pp[par].buf],
                    bias=groups[gi]["bias"][0:n, :])

            def pv(i):
                gi, t, n = tiles[i]
                sl = slots[gi]
                par = i % 2
                for m in range(2):
                    pw = w_p[m][par]
                    mm(banks[4 + m][:, 0:nq], vring[sl][0:n, t * 128:(t + 1) * 128], pw.ap[0:n, 0:nq],
                       i == 0, i == NTL - 1, [b_kv[sl], pw.buf], [b_bank[4 + m]])
                    mm(banks[6 + m][:, 0:nq], ones_b[0:n, :], pw.ap[0:n, 0:nq],
                       i == 0, i == NTL - 1, [b_const, pw.buf], [b_bank[6 + m]])

            qk(0)
            for i in range(NTL):
                ex(i)
                if i + 1 < NTL:
                    qk(i + 1)
                pv(i)
            t1, t2 = w_t[0], w_t[1]
            P.op("dve", lambda e, o=t1.ap[:, 0:nq], i=banks[6][:, 0:nq]: e.reciprocal(out=o, in_=i), [b_bank[6]], [t1.buf])
            tt("dve", t1.ap[:, 0:nq], t1.ap[:, 0:nq], banks[4][:, 0:nq], ALU.mult, [t1.buf, b_bank[4]], [t1.buf])
            P.op("dve", lambda e, o=t2.ap[:, 0:nq], i=banks[7][:, 0:nq]: e.reciprocal(out=o, in_=i), [b_bank[7]], [t2.buf])
            tt("dve", t2.ap[:, 0:nq], t2.ap[:, 0:nq], banks[5][:, 0:nq], ALU.mult, [t2.buf, b_bank[5]], [t2.buf])
            stt("dve", t1.ap[:, 0:nq], t2.ap[:, 0:nq], nlam[:, 0:1], t1.ap[:, 0:nq], ALU.mult, ALU.add,
                [t1.buf, t2.buf, b_const], [t1.buf])
            s = nxt("sq", 2)
            act(sqt[s][:, 0:nq], t1.ap[:, 0:nq], AF.Square, [t1.buf], [b_sq[s]])
            mm(banks[0][:, 0:nq], ones_b[:], sqt[s][:, 0:nq], True, True, [b_sq[s], b_const], [b_bank[0]])
            act(t2.ap[:, 0:nq], banks[0][:, 0:nq], AF.Ln, [b_bank[0], b_const], [t2.buf], bias=epsb[:, 0:1], scale=1.0 / 128)
            act(t2.ap[:, 0:nq], t2.ap[:, 0:nq], AF.Exp, [t2.buf], [t2.buf], scale=-0.5)
            stt("dve", H[:, 16 + h, q0:q0 + nq], t1.ap[:, 0:nq], gsub[:, 0:1], t2.ap[:, 0:nq], ALU.mult, ALU.mult,
                [t1.buf, t2.buf, b_const], [b_H[16 + h]])

        rt = tiles[::-1]

        def v3(ap, n):
            return ap[0:n, :].rearrange("p (m q) -> p m q", m=2)[:, :, 0:nq]

        for hp in (range(4) if AM == "" else (range(1) if AM.startswith("sb") else range(0))):
            hh2 = (2 * hp, 2 * hp + 1)
            slots = [{}, {}]
            P.op("dve", lambda e: e.memset(w_cp.ap[:, :], 0.0), [], [w_cp.buf])

            def QK(i):
                gi, t, n = rt[i]
                for s_ in range(2):
                    h = hh2[s_]
                    if gi not in slots[s_]:
                        slots[s_][gi] = load(gi, 1, h)
                    sl = slots[s_][gi]
                    zb = 2 * (i % 2) + s_
                    mk = groups[gi]["mask_s"].get(t)
                    mm(banks[zb][0:n, 0:nq], kring[sl][:, t * 128:t * 128 + n], H[:, 8 + h, q0:q0 + nq], True, False,
                       [b_kv[sl], b_H[8 + h]], [b_bank[zb]])
                    if mk is not None:
                        mm(banks[zb][0:n, 0:nq], ident_b[0:n, 0:n], masks[0:n, mk:mk + nq], False, False,
                           [b_const], [b_bank[zb]])

            def ES(i):
                gi, t, n = rt[i]
                p = i % 2
                act(v3(w_e2.ap, n), v3(pairs[p], n), AF.Exp, [b_bank[2 * p], b_bank[2 * p + 1], b_const], [w_e2.buf],
                    bias=groups[gi]["bias"][0:n, :])
                act(v3(w_sp2[p].ap, n), v3(w_e2.ap, n), AF.Ln, [w_e2.buf], [w_sp2[p].buf], bias=1.0, scale=1.0)

            def TR(i):
                gi, t, n = rt[i]
                p = i % 2
                for s_ in range(2):
                    zb = 2 * p + s_
                    spv_ = w_sp2[p].ap[0:n, s_ * 512:s_ * 512 + nq]
                    mm(banks[zb][0:n, 0:nq], ntri_b[0:n, 0:n], spv_, False, True, [w_sp2[p].buf, b_const], [b_bank[zb]])
                    if i < NTL - 1:
                        mm(banks[6 + s_][:, 0:nq], ones_b[0:n, :], spv_, i == 0, i == NTL - 2,
                           [w_sp2[p].buf, b_const], [b_bank[6 + s_]])

            def DV(i):
                gi, t, n = rt[i]
                p = i % 2
                tt("dve", v3(w_tm2[p].ap, n), v3(pairs[p], n), v3(w_cp.ap, n), ALU.subtract,
                   [b_bank[2 * p], b_bank[2 * p + 1], w_cp.buf], [w_tm2[p].buf])
                if i < NTL - 1:
                    cp("dve", v3(w_cp.ap, 128), v3(pairs[3], 128), [b_bank[6], b_bank[7]], [w_cp.buf])

            def AX(i):
                gi, t, n = rt[i]
                p = i % 2
                act(v3(w_a2[p].ap, n), v3(w_tm2[p].ap, n), AF.Exp, [w_tm2[p].buf, b_const], [w_a2[p].buf],
                    bias=groups[gi]["bias"][0:n, :])

            def PV(i):
                gi, t, n = rt[i]
                p = i % 2
                for s_ in range(2):
                    sl = slots[s_][gi]
                    mm(banks[4 + s_][:, 0:nq], vring[sl][0:n, t * 128:(t + 1) * 128],
                       w_a2[p].ap[0:n, s_ * 512:s_ * 512 + nq], i == 0, i == NTL - 1,
                       [b_kv[sl], w_a2[p].buf], [b_bank[4 + s_]])

            QK(0)
            if NTL > 1:
                QK(1)
            ES(0)
            TR(0)
            DV(0)
            for j in range(NTL):
                if j + 2 < NTL:
                    QK(j + 2)
                if j + 1 < NTL:
                    ES(j + 1)
                    TR(j + 1)
                    DV(j + 1)
                AX(j)
                PV(j)
            for s_ in range(2):
                h = hh2[s_]
                cp("act", H[:, 24 + h, q0:q0 + nq], banks[4 + s_][:, 0:nq], [b_bank[4 + s_]], [b_H[24 + h]])

    def merge_wo(nt):
        for o in range(16):
            i = wload(wmg[o], 6144)
            wt = wsl[i][:, 0:6144].rearrange("p (k n) -> p k n", n=128)
            par = 4 * (o % 2)
            for k in range(8):
                mm(banks[par][:, 0:nt], wt[:, 32 + k, :], H[:, 16 + k, 0:nt], k == 0, k == 7,
                   [b_w[i], b_H[16 + k]], [b_bank[par]])
            for k in range(KC):
                mm(banks[par + 1][:, 0:nt], wt[:, k, :], xn[:, k, 0:nt], k == 0, k == KC - 1,
                   [b_w[i], b_xn[k]], [b_bank[par + 1]])
            for k in range(8):
                mm(banks[par + 2][:, 0:nt], wt[:, 40 + k, :], H[:, 24 + k, 0:nt], k == 0, k == 7,
                   [b_w[i], b_H[24 + k]], [b_bank[par + 2]])
            for k in range(KC):
                mm(banks[par + 3][:, 0:nt], wt[:, 16 + k, :], xn[:, k, 0:nt], k == 0, k == KC - 1,
                   [b_w[i], b_xn[k]], [b_bank[par + 3]])
            sa, sbw = w_t[0], w_t[1]
            act(sa.ap[:, 0:nt], banks[par + 1][:, 0:nt], AF.Sigmoid, [b_bank[par + 1]], [sa.buf])
            act(sbw.ap[:, 0:nt], banks[par + 3][:, 0:nt], AF.Sigmoid, [b_bank[par + 3]], [sbw.buf])
            tt("dve", sa.ap[:, 0:nt], sa.ap[:, 0:nt], banks[par][:, 0:nt], ALU.mult, [sa.buf, b_bank[par]], [sa.buf])
            tt("dve", sbw.ap[:, 0:nt], sbw.ap[:, 0:nt], banks[par + 2][:, 0:nt], ALU.mult, [sbw.buf, b_bank[par + 2]],
               [sbw.buf])
            tt("dve", H[:, o, 0:nt], sa.ap[:, 0:nt], sbw.ap[:, 0:nt], ALU.add, [sa.buf, sbw.buf], [b_H[o]])
        for o in range(16):
            i = wload(wot[o], 2048)
            wt = wsl[i][:, 0:2048].rearrange("p (k n) -> p k n", n=128)
            bk = o % 4
            for k in range(KC):
                mm(banks[bk][:, 0:nt], wt[:, k, :], H[:, k, 0:nt], k == 0, k == KC - 1, [b_w[i], b_H[k]], [b_bank[bk]])
            cp("act", y1[:, o, 0:nt], banks[bk][:, 0:nt], [b_bank[bk]], [b_y1[o]])

    c32 = [sb(f"c32_{i}", [128, 512], F32) for i in range(2)]
    cb16 = [sb(f"cb16_{i}", [128, 512], BF16) for i in range(2)]
    ckt = [sb(f"ckt_{i}", [128, 4, 128], BF16) for i in range(2)]
    b_c32 = [B("c32_0"), B("c32_1")]
    b_cb16 = [B("cb16_0"), B("cb16_1")]
    b_ckt = [B("ckt0"), B("ckt1")]
    rr["cc"] = 0

    def cache_units():
        for sq_ in range(4):
            for br, (ck, cv) in enumerate(((ck_a, cv_a), (ck_b, cv_b))):
                for t in range(8):
                    for isv, src in ((0, ck), (1, cv)):
                        for hh in range(2):
                            yield (sq_, br, t, isv, src, hh)

    def cache_unit(u):
        sq_, br, t, isv, src, hh = u
        s = nxt("cc", 2)
        dma("pool", c32[s][:], src[sq_, t * 128:(t + 1) * 128, hh * 512:(hh + 1) * 512], [], [b_c32[s]], b_c32[s])
        cp("dve", cb16[s][:], c32[s][:], [b_c32[s]], [b_cb16[s]])
        if isv:
            dma("pool", v_s[br][sq_, 4 * hh:4 * hh + 4, :, t * 128:(t + 1) * 128].rearrange("h p e -> p h e"),
                cb16[s][:].rearrange("p (h e) -> p h e", e=128), [b_cb16[s]], [dbuf(("cv", sq_, br))], b_cb16[s])
        else:
            bv = banks[7].bitcast(BF16)
            for j in range(4):
                P.op("pe", lambda e, o=bv[:, j * 128:(j + 1) * 128], i=cb16[s][:, j * 128:(j + 1) * 128]:
                     e.transpose(o, i, ident_b[:]), [b_cb16[s], b_const], [b_bank[7]])
            cp("dve", ckt[s][:, :, :], bv[:, 0:512].rearrange("p (j t) -> p j t", j=4), [b_bank[7]], [b_ckt[s]])
            dma("pool", kt_s[br][sq_, 4 * hh:4 * hh + 4, :, t * 128:(t + 1) * 128].rearrange("h p t -> p h t"),
                ckt[s][:, :, :], [b_ckt[s]], [dbuf(("ck", sq_, br))], b_ckt[s])

    cache_iter = [iter(cache_units()) if DO_SMP else iter(())]

    def cache_hook():
        u = next(cache_iter[0], None)
        if u is not None:
            cache_unit(u)

    def cache_flush():
        for u in cache_iter[0]:
            cache_unit(u)

    def prompt_kt_dst(tok0):
        return lambda br, hh: [(kt_p[br][4 * hh:4 * hh + 4, :, tok0:tok0 + NT].rearrange("h p t -> p h t"), 0, NT)]

    def prompt_v_dst(tok0):
        def f(br, hh, sub):
            tile = tok0 // 128 + sub
            return [(v_p[br][4 * hh:4 * hh + 4, :, tile * 128:(tile + 1) * 128].rearrange("h p e -> p h e"), 0, 128)]
        return f

    def front(kind, src, row0, nt, pos_row0, tok0, outs, out_row0, kt_dst, v_dst):
        load_x(src, row0, nt)
        norm_to_xn(0, nt)
        ffn(1, nt, cache_hook if kind == "aux" else None)
        post_residual(1, nt, 0.5)
        norm_to_xn(2, nt)
        qkv_proj(kind, nt, pos_row0, tok0, outs, out_row0, kt_dst, v_dst)

    def back(dst, row0, nt):
        merge_wo(nt)
        post_residual(3, nt, 1.0)
        norm_to_xn(4, nt)
        ffn(2, nt)
        post_residual(5, nt, 0.5)
        store_y(dst, row0, nt)

    STOP = cfg.get("stop", "")
    load_consts()
    if STOP == "consts":
        return nc, P, es
    convert_weights()
    if STOP == "weights":
        return nc, P, es
    for b in range(NB_AUX):
        front("aux", xa, b * NT, NT, b * NT, b * NT, None, 0, prompt_kt_dst(b * NT), prompt_v_dst(b * NT))
    cache_flush()
    if STOP == "front_aux":
        return nc, P, es
    for b in range(NB_OWN):
        tok0 = 4096 + b * NT
        front("own", xo, b * NT, NT, 4096 + b * NT, tok0, kv_o, b * NT, prompt_kt_dst(tok0), prompt_v_dst(tok0))
        if STOP == "front_own":
            return nc, P, es
        groups = []
        for gi in list(range(NB_AUX)) + [8 + j for j in range(b + 1)]:
            own = gi >= 8
            k0 = gi * 512
            keyb = (lambda br, gi=gi, own=own: [dbuf(("kt", "own" if own else "aux", br, gi * 512, hh)) for hh in range(2)]
                    + [dbuf(("v", "own" if own else "aux", br, gi * 512, hh, sub)) for hh in range(2) for sub in range(4)])
            groups.append(dict(
                nk=512,
                kt=lambda br, h, k0=k0: kt_p[br][h, :, k0:k0 + 512],
                v=lambda br, h, k0=k0: v_p[br][h, :, (k0 // 128) * 128:(k0 // 128 + 4) * 128],
                bufs=keyb,
                bias=zerob if own else auxb,
                mask_d=({t: t * 512 for t in range(4)} if gi == 8 + b else {}),
                mask_s=({t: t * 512 for t in range(4)} if gi == 8 + b else {}),
            ))
        attention(NT, groups)
        if STOP == "attn":
            return nc, P, es
        back(y_o, b * NT, NT)
    if DO_SMP:
        nt = 256

        def s_kt_dst(br, hh):
            return [(kt_s[br][i, 4 * hh:4 * hh + 4, :, 1024:1088].rearrange("h p t -> p h t"), 64 * i, 64 * i + 64)
                    for i in range(4)]

        def s_v_dst(br, hh, sub):
            return [(v_s[br][2 * sub + j, 4 * hh:4 * hh + 4, 0:64, 1024:1152].rearrange("h p e -> p h e"),
                     64 * j, 64 * j + 64) for j in range(2)]

        front("smp", xs, 0, nt, 8192, 0, kv_s, 0, s_kt_dst, s_v_dst)
        for i in range(4):
            groups = []
            for gi in range(3):
                nk = 512 if gi < 2 else 64
                k0 = gi * 512
                if gi < 2:
                    keyb = lambda br, i=i: [dbuf(("ck", i, br)), dbuf(("cv", i, br))]
                else:
                    keyb = lambda br: ([dbuf(("kt", "smp", br, 0, hh)) for hh in range(2)]
                                       + [dbuf(("v", "smp", br, 0, hh, sub)) for hh in range(2) for sub in range(2)])
                groups.append(dict(
                    nk=nk,
                    kt=lambda br, h, i=i, k0=k0, nk=nk: kt_s[br][i, h, :, k0:k0 + nk],
                    v=(lambda br, h, i=i, k0=k0: v_s[br][i, h, :, k0:k0 + 512]) if gi < 2 else
                      (lambda br, h, i=i: v_s[br][i, h, 0:64, 1024:1152]),
                    bufs=keyb, bias=zerob, mask_d={}, mask_s=({0: 0} if gi == 2 else {}),
                ))
            attention(64, groups, q0=64 * i)
        back(y_s, 0, nt)

    return nc, P, es


def emit_program(nc, P, es):
    P.finalize()
    with es:
        sems = {e: es.enter_context(nc.semaphore(f"s_{e}")) for e in ("pe", "act", "dve", "pool")}
        for i, b in enumerate(P.dma_bufs):
            b.sem = es.enter_context(nc.semaphore(f"d{i}"))
        block = es.enter_context(nc.Block())

        def run(name, e):
            waited = {}
            for op in P.ops[name]:
                need = {}
                for d in op.deps:
                    if d.dma is not None:
                        key, sem, val = id(d.dma), d.dma.sem, d.val
                    else:
                        if d.eng == name and name == "pe":
                            continue
                        key, sem, val = d.eng, sems[d.eng], d.val
                    if key not in need or need[key][1] < val:
                        need[key] = (sem, val)
                for key, (sem, val) in need.items():
                    if waited.get(key, 0) < val:
                        e.wait_ge(sem, val)
                        waited[key] = val
                ins = op.fn(e)
                if op.dma is not None:
                    ins.then_inc(op.dma.sem, 16)
                elif op.mark:
                    ins.then_inc(sems[name], 1)
            if name == "sp":
                for b in P.dma_bufs:
                    e.wait_ge(b.sem, 16 * b.ndma)

        @block.sync
        def _(e):
            run("sp", e)

        @block.gpsimd
        def _(e):
            run("pool", e)

        @block.tensor
        def _(e):
            run("pe", e)

        @block.scalar
        def _(e):
            run("act", e)

        @block.vector
        def _(e):
            run("dve", e)
    return nc


def _rope_tables(pos):
    half = 8
    inv_freq = np.power(np.float32(500000.0), -(np.arange(0, 16, 2, dtype=np.float32) / np.float32(16.0))).astype(np.float32)
    ang = pos.astype(np.float32)[:, None] * inv_freq[None, :]
    c = np.cos(ang).astype(np.float32)
    s_ = np.sin(ang).astype(np.float32)
    c = np.ascontiguousarray(np.broadcast_to(c[:, None, :], (pos.shape[0], 16, 8))).reshape(pos.shape[0], 128)
    s_ = np.ascontiguousarray(np.broadcast_to(s_[:, None, :], (pos.shape[0], 16, 8))).reshape(pos.shape[0], 128)
    return c, s_


def _consts():
    ident = np.eye(128, dtype=np.float32)
    j = np.arange(128)[:, None]
    sidx = np.arange(128)[None, :]
    ntri = np.where(j >= sidx, -1.0, 0.0).astype(np.float32)
    k = np.arange(128)[:, None]
    q = np.arange(512)[None, :]
    md = np.zeros((128, 4, 512), np.float32)
    ms = np.zeros((128, 4, 512), np.float32)
    for t in range(4):
        kk = 128 * t + k
        md[:, t, :] = np.where((kk // 64) <= (q // 64), 0.0, NEG_D)
        ms[:, t, :] = np.where(kk < q, 0.0, NEG_S)
    return ident, ntri, md.reshape(128, 2048), ms.reshape(128, 2048)


def make_in_maps(inp):
    f32 = np.float32
    g = lambda k: np.asarray(inp[k], dtype=f32)
    xp, xsm = g("x_prompt"), g("x_sample")
    ident, ntri, md, ms = _consts()
    gains = np.stack([g(n)[0].reshape(16, 128).T for n in (
        "norm_ffn1_pre", "norm_ffn1_post", "norm_mix_pre", "norm_mix_post", "norm_ffn2_pre", "norm_ffn2_post")], axis=1)
    gains = np.ascontiguousarray(gains.reshape(128, 96))
    subg = np.ascontiguousarray(g("subln_g")[0].reshape(128, 1))
    lamv = np.concatenate([g("lam_q1")[0], g("lam_k1")[0], g("lam_q2")[0], g("lam_k2")[0]])[None, :]
    lamv = np.ascontiguousarray(np.broadcast_to(lamv, (128, 256)))
    shared = dict(
        w_in=g("w_in")[0], w_up_a=g("w_up_a")[0], w_up_b=g("w_up_b")[0], w_o=g("w_o")[0],
        f1g=g("ffn1_w_gate")[0], f1u=g("ffn1_w_up")[0], f1d=g("ffn1_w_down")[0],
        f2g=g("ffn2_w_gate")[0], f2u=g("ffn2_w_up")[0], f2d=g("ffn2_w_down")[0],
        gains=gains, subg=subg, lamv=lamv, ident=ident, ntri=ntri, mask_d=md, mask_s=ms,
    )
    cka, cva, ckb, cvb = g("cache_diff_k")[0], g("cache_diff_v")[0], g("cache_sb_k")[0], g("cache_sb_v")[0]
    pos_s = (1024 + np.arange(64)).astype(np.int64)
    maps = []
    for c in range(8):
        seq, half = c // 2, c % 2
        pos = np.concatenate([np.arange(4096), half * 4096 + np.arange(4096), np.tile(pos_s, 4)])
        ct, st_ = _rope_tables(pos)
        m = dict(shared)
        m.update(
            xa=np.ascontiguousarray(xp[seq, 0:4096]),
            xo=np.ascontiguousarray(xp[seq, half * 4096:(half + 1) * 4096]),
            xs=np.ascontiguousarray(xsm[4 * c:4 * c + 4].reshape(256, D)),
            cos_t=ct, sin_t=st_,
            ck_a=np.ascontiguousarray(cka[4 * c:4 * c + 4].reshape(4, 1024, 1024)),
            cv_a=np.ascontiguousarray(cva[4 * c:4 * c + 4].reshape(4, 1024, 1024)),
            ck_b=np.ascontiguousarray(ckb[4 * c:4 * c + 4].reshape(4, 1024, 1024)),
            cv_b=np.ascontiguousarray(cvb[4 * c:4 * c + 4].reshape(4, 1024, 1024)),
            auxb=np.full((128, 1), 0.0 if half == 1 else NEG_D, f32),
        )
        maps.append(m)
    return maps


_PROG_CACHE = {}


def run_cores(inp, cfg=None):
    cfg = dict(CFG) if cfg is None else cfg
    nc, P, es = build_program(cfg)
    emit_program(nc, P, es)
    maps = make_in_maps(inp)
    n = cfg.get("ncores", 8)
    res = run_bass_kernel_spmd(nc, maps[:n], core_ids=list(range(n)))
    return res.results


def kernel(**inp):
    r = run_cores(inp)
    f32 = np.float32
    y_p = np.empty((4, 8192, D), f32)
    y_s = np.empty((32, 64, D), f32)
    pk = [np.empty((1, 4, 8192, 1024), f32) for _ in range(4)]
    sk = [np.empty((1, 32, 64, 1024), f32) for _ in range(4)]
    for c in range(8):
        seq, half = c // 2, c % 2
        sl = slice(half * 4096, (half + 1) * 4096)
        y_p[seq, sl] = r[c]["y_o"]
        y_s[4 * c:4 * c + 4] = np.asarray(r[c]["y_s"]).reshape(4, 64, D)
        for i, n in enumerate(("ka", "va", "kb", "vb")):
            pk[i][0, seq, sl] = r[c][n + "_o"]
            sk[i][0, 4 * c:4 * c + 4] = np.asarray(r[c][n + "_s"]).reshape(4, 64, 1024)
    return (y_p, y_s,
            pk[0].reshape(1, 4, 8192, 8, 2, 64), pk[1].reshape(1, 4, 8192, 8, 128),
            pk[2].reshape(1, 4, 8192, 8, 128), pk[3].reshape(1, 4, 8192, 8, 128),
            sk[0].reshape(1, 32, 64, 8, 2, 64), sk[1].reshape(1, 32, 64, 8, 128),
            sk[2].reshape(1, 32, 64, 8, 128), sk[3].reshape(1, 32, 64, 8, 128))
```
